# Optimizing a Trainium2 kernel written in Bass

```python
import jax, jax.numpy as jnp
from jax import lax
import numpy as np

D_MODEL = 1024
BATCH = 16
SEQ = 256
DEPTH = 2
DEC_BATCH = 8
DEC_SEQ = 2048
PAST_LEN = 512

GRID_W = 64
CHUNK = 64
H_A = 4
DK_A = 128
DV_A = 128
D_A = H_A * DV_A
H_B = 4
DK_B = 128
DV_B = 128
D_B = H_B * DV_B
D_MIX = D_A + D_B
D_FF = 2816
ROPE_BASE = 10000.0
EPS = 1e-6
GATE_FLOOR = 1e-12
IN_SIZES = [H_A * DK_A, H_A * DK_A, H_A * DK_A, D_A, D_A, H_B * DK_B, H_B * DK_B, D_B, D_B]
IN_OFFSETS = [int(v) for v in np.cumsum(IN_SIZES)[:-1]]
D_IN = int(sum(IN_SIZES))

kernel_name = "hybrid_hgrn2_retention_diffusion_step"


def rmsnorm(x, g):
    x = x.astype(jnp.float32)
    return x * lax.rsqrt(jnp.mean(x * x, axis=-1, keepdims=True) + EPS) * g


def heads(x, h):
    b, l, _ = x.shape
    return x.reshape(b, l, h, -1).transpose(0, 2, 1, 3)


def head_norm(o, g, center):
    if center:
        o = o - jnp.mean(o, axis=-1, keepdims=True)
    o = o * lax.rsqrt(jnp.mean(o * o, axis=-1, keepdims=True) + EPS)
    b, h, l, dv = o.shape
    return o.transpose(0, 2, 1, 3).reshape(b, l, h * dv) * g


def grid_positions(n_tokens):
    rows_n = n_tokens // GRID_W
    rr, cc = jnp.meshgrid(jnp.arange(rows_n, dtype=jnp.float32), jnp.arange(GRID_W, dtype=jnp.float32), indexing="ij")
    return rr.reshape(-1), cc.reshape(-1)


def rope_2d(x, rows, cols):
    half = x.shape[-1] // 2
    quarter = half // 2
    inv = ROPE_BASE ** (-jnp.arange(quarter, dtype=jnp.float32) / quarter)

    def rot(xb, pos):
        ang = pos[:, None] * inv
        cs, sn = jnp.cos(ang), jnp.sin(ang)
        x1, x2 = xb[..., :quarter], xb[..., quarter:]
        return jnp.concatenate([x1 * cs - x2 * sn, x2 * cs + x1 * sn], axis=-1)

    return jnp.concatenate([rot(x[..., :half], rows), rot(x[..., half:], cols)], axis=-1)


def chunk_scan(q, k, v, logf, s0):
    b_, h_, l_, _ = q.shape
    n = l_ // CHUNK

    def blocks(t):
        return t.reshape(b_, h_, n, CHUNK, t.shape[-1]).transpose(2, 0, 1, 3, 4)

    causal = jnp.tril(jnp.ones((CHUNK, CHUNK), dtype=bool))[:, :, None]
    scalar_decay = logf.shape[-1] == 1

    def step(S, blk):
        qc, kc, vc, gc = blk
        bcum = jnp.cumsum(gc, axis=-2)
        diff = bcum[..., :, None, :] - bcum[..., None, :, :]
        decay = jnp.where(causal, jnp.exp(jnp.where(causal, diff, 0.0)), 0.0)
        if scalar_decay:
            A = jnp.einsum("bhtd,bhsd->bhts", qc, kc) * decay[..., 0]
        else:
            A = jnp.sum(qc[..., :, None, :] * decay * kc[..., None, :, :], axis=-1)
        o = jnp.einsum("bhts,bhsv->bhtv", A, vc) + jnp.einsum("bhtd,bhdv->bhtv", qc * jnp.exp(bcum), S)
        b_last = bcum[..., -1:, :]
        S = jnp.exp(b_last)[..., 0, :, None] * S + jnp.einsum("bhsd,bhsv->bhdv", kc * jnp.exp(b_last - bcum), vc)
        return S, o

    S, o = lax.scan(step, s0.astype(jnp.float32), (blocks(q), blocks(k), blocks(v), blocks(logf)))
    return o.transpose(1, 2, 0, 3, 4).reshape(b_, h_, l_, -1), S


def bidirectional_scan(q, k_f, k_b, v, g_f, g_b, s0):
    o_f, S_f = chunk_scan(q, k_f, v, g_f, s0[:, 0])
    flip = lambda t: jnp.flip(t, axis=2)
    o_b, S_b = chunk_scan(flip(q), flip(k_b), flip(v), flip(g_b), s0[:, 1])
    return o_f + flip(o_b), jnp.stack([S_f, S_b], axis=1)


def hgrn_gate(z, lb):
    f = lb + (1.0 - lb) * jax.nn.sigmoid(z)
    logf = jnp.log(jnp.maximum(f, GATE_FLOOR))
    k = (1.0 - lb) * jax.nn.sigmoid(-z)
    return heads(logf, H_A), heads(k, H_A)


def token_mixers(h, p, s0_h, s0_r, pos):
    b_, l_, _ = h.shape
    proj = h @ p["w_in"]
    qa, zf, zb, ia, ga, qb, kb, vb, gb = jnp.split(proj, IN_OFFSETS, axis=-1)
    lg_f, k_f = hgrn_gate(zf, p["lb"][0])
    lg_b, k_b = hgrn_gate(zb, p["lb"][1])
    o_a, S_a = bidirectional_scan(heads(jax.nn.silu(qa), H_A), k_f, k_b, heads(ia, H_A), lg_f, lg_b, s0_h)
    o_a = head_norm(o_a, p["hgrn_norm"], center=False) * jax.nn.silu(ga)
    qh = heads(qb, H_B)
    kh = heads(kb, H_B) * (DK_B ** -0.5)
    if pos is not None:
        qh = rope_2d(qh, *pos)
        kh = rope_2d(kh, *pos)
    log_gamma = jax.nn.log_sigmoid(p["ret_logit"].astype(jnp.float32))
    gam_f = jnp.broadcast_to(log_gamma[0][None, :, None, None], (b_, H_B, l_, 1))
    gam_b = jnp.broadcast_to(log_gamma[1][None, :, None, None], (b_, H_B, l_, 1))
    o_b, S_b = bidirectional_scan(qh, kh, kh, heads(vb, H_B), gam_f, gam_b, s0_r)
    o_b = head_norm(o_b, p["ret_norm"], center=True) * jax.nn.silu(gb)
    return jnp.concatenate([o_a, o_b], axis=-1) @ p["w_out"], S_a, S_b


def layer(x, cond, s0_h, s0_r, pos, p):
    mod = jax.nn.silu(cond.astype(jnp.float32)) @ p["ada_w"] + p["ada_b"]
    mod = mod.reshape(cond.shape[:-1] + (6, D_MODEL))
    sh_m, sc_m, g_m, sh_f, sc_f, g_f = [mod[..., i, :][..., None, :] for i in range(6)]
    h = rmsnorm(x, p["norm_mix"]) * (1.0 + sc_m) + sh_m
    y, S_a, S_b = token_mixers(h, p, s0_h, s0_r, pos)
    x = x + g_m * y
    h = rmsnorm(x, p["norm_ffn"]) * (1.0 + sc_f) + sh_f
    gate, up = jnp.split(h @ p["w_up"], 2, axis=-1)
    x = x + g_f * ((jax.nn.silu(gate) * up) @ p["w_down"])
    return x, S_a, S_b


def setup_inputs(seed: int = 0) -> dict:
    key = jax.random.key(seed)
    ks = jax.random.split(key, 20)
    nrm = lambda k, shape, s: jax.random.normal(k, shape, jnp.float32) * s
    base_logit = jnp.log(2.0 ** (5.0 + jnp.arange(H_B, dtype=jnp.float32)) - 1.0)
    return {
        "x_prompt": nrm(ks[0], (BATCH, SEQ, D_MODEL), 1.0),
        "x_sample": nrm(ks[1], (DEC_BATCH, DEC_SEQ, D_MODEL), 1.0),
        "state_hgrn": nrm(ks[2], (DEC_BATCH, DEPTH, 2, H_A, DK_A, DV_A), 0.5),
        "state_ret": nrm(ks[3], (DEC_BATCH, DEPTH, 2, H_B, DK_B, DV_B), 0.5),
        "c": nrm(ks[4], (DEC_BATCH, D_MODEL), 1.0),
        "c_ctx": nrm(ks[5], (D_MODEL,), 1.0),
        "ada_w": nrm(ks[6], (DEPTH, D_MODEL, 6 * D_MODEL), 0.5 * D_MODEL ** -0.5),
        "ada_b": nrm(ks[7], (DEPTH, 6 * D_MODEL), 0.02),
        "norm_mix": 1.0 + nrm(ks[8], (DEPTH, D_MODEL), 0.05),
        "norm_ffn": 1.0 + nrm(ks[9], (DEPTH, D_MODEL), 0.05),
        "w_in": nrm(ks[10], (DEPTH, D_MODEL, D_IN), D_MODEL ** -0.5),
        "hgrn_lb_logits": nrm(ks[11], (DEPTH, 2, H_A * DK_A), 1.0),
        "hgrn_norm": 1.0 + nrm(ks[12], (DEPTH, D_A), 0.05),
        "ret_decay_logit": base_logit[None, None, :] + nrm(ks[13], (DEPTH, 2, H_B), 0.1),
        "ret_norm": 1.0 + nrm(ks[14], (DEPTH, D_B), 0.05),
        "w_out": nrm(ks[15], (DEPTH, D_MIX, D_MODEL), D_MIX ** -0.5),
        "w_up": nrm(ks[16], (DEPTH, D_MODEL, 2 * D_FF), D_MODEL ** -0.5),
        "w_down": nrm(ks[17], (DEPTH, D_FF, D_MODEL), D_FF ** -0.5),
        "norm_final": 1.0 + nrm(ks[18], (D_MODEL,), 0.05),
    }


def reference(x_prompt, x_sample, state_hgrn, state_ret, c, c_ctx, ada_w, ada_b, norm_mix, norm_ffn, w_in,
              hgrn_lb_logits, hgrn_norm, ret_decay_logit, ret_norm, w_out, w_up, w_down, norm_final):
    pl = jax.nn.softmax(hgrn_lb_logits.astype(jnp.float32), axis=0)
    lower_bounds = jnp.clip(jnp.cumsum(pl, axis=0) - pl[0:1], 0.0, 1.0)
    xp = x_prompt.astype(jnp.float32)
    xs = x_sample.astype(jnp.float32)
    bp = xp.shape[0]
    zeros_h = jnp.zeros((bp, 2, H_A, DK_A, DV_A), jnp.float32)
    zeros_r = jnp.zeros((bp, 2, H_B, DK_B, DV_B), jnp.float32)
    cond_ctx = c_ctx[None, :]
    pos = grid_positions(xs.shape[1])
    new_h, new_r = [], []
    for l in range(DEPTH):
        p = {"ada_w": ada_w[l], "ada_b": ada_b[l], "norm_mix": norm_mix[l], "norm_ffn": norm_ffn[l],
             "w_in": w_in[l], "lb": lower_bounds[l], "hgrn_norm": hgrn_norm[l], "ret_logit": ret_decay_logit[l],
             "ret_norm": ret_norm[l], "w_out": w_out[l], "w_up": w_up[l], "w_down": w_down[l]}
        xp, S_a, S_b = layer(xp, cond_ctx, zeros_h, zeros_r, None, p)
        new_h.append(S_a)
        new_r.append(S_b)
        xs, _, _ = layer(xs, c, state_hgrn[:, l], state_ret[:, l], pos, p)
    y_prompt = rmsnorm(xp, norm_final).astype(x_prompt.dtype)
    y_sample = rmsnorm(xs, norm_final).astype(x_sample.dtype)
    new_state_hgrn = jnp.stack(new_h, axis=1).astype(state_hgrn.dtype)
    new_state_ret = jnp.stack(new_r, axis=1).astype(state_ret.dtype)
    return (y_prompt, y_sample, new_state_hgrn, new_state_ret)
```

```python
import threading
import numpy as np
from contextlib import ExitStack
import concourse.bass as bass
import concourse.mybir as mybir
from concourse.bass_utils import run_bass_kernel_spmd

F32 = mybir.dt.float32
BF16 = mybir.dt.bfloat16
AF = mybir.ActivationFunctionType
ALU = mybir.AluOpType

NCORES = 8
L = 2
D = 1024
KC = 8
H = 4
DFF = 2816
NFF = 22
TB = 512
NBLK = 5
T = TB * NBLK
SEQP = 256
SEQS = 2048
EPS = 1e-6
LOG_FLOOR = float(np.log(1e-12))
LN_KSCALE = float(np.log(128.0 ** -0.5))

P_COND = 0
P_ADAB = 16
P_NMIX = 112
P_NFFN = 128
P_NFIN = 144
P_LB = 152
P_HN = 168
P_RN = 176
P_RD = 184
NPAR = 200
C_M32 = 0
C_M64 = 512
C_MASK = 1024
C_ID = 1152
C_RM = 1280
C_TP1 = 1408
C_TM = 1472
NCST = 1536

SLOT = 3072
NSLOT = 4
SAME_ENGINE_SYNC = True
LIST_SCHED = True


class Buf:
    __slots__ = ("ap", "w", "r", "excl")

    def __init__(self, ap, excl=False):
        self.ap = ap
        self.w = {}
        self.r = {}
        self.excl = excl


class Sched:
    def __init__(self, nc, es):
        self.nc = nc
        self.eng = {"pe": nc.tensor, "act": nc.scalar, "dve": nc.vector, "pool": nc.gpsimd, "sp": nc.sync}
        self.sems = {}
        self.val = {}
        for e in self.eng:
            self.sems[e] = es.enter_context(nc.semaphore("s_" + e))
            self.val[e] = 0
        self.dpool = {"sp": [], "pool": []}
        for q, n in (("sp", 12), ("pool", 32)):
            for i in range(n):
                k = "d_%s%d" % (q, i)
                self.sems[k] = es.enter_context(nc.semaphore(k))
                self.val[k] = 0
                self.dpool[q].append(k)
        self.dnext = {"sp": 0, "pool": 0}
        self.seen = {e: {} for e in self.eng}
        self.nops = 0
        self.hooks = {}
        self.efree = {e: 0.0 for e in self.eng}
        self.tfin = {}

    def _wait(self, e, key, v):
        if self.seen[e].get(key, 0) >= v:
            return
        self.eng[e].wait_ge(self.sems[key], v)
        self.seen[e][key] = v

    def _dep(self, e, k, v):
        if k == e and (e == "pe" or not SAME_ENGINE_SYNC):
            return
        self._wait(e, k, v)

    def _deps(self, e, reads, writes):
        for b in reads:
            for k, v in b.w.items():
                self._dep(e, k, v)
            if b.excl:
                for k, v in b.r.items():
                    if k != e:
                        self._dep(e, k, v)
        for b in writes:
            for k, v in b.w.items():
                self._dep(e, k, v)
            for k, v in b.r.items():
                self._dep(e, k, v)

    def _mark(self, key, v, reads, writes):
        for b in writes:
            if b.r:
                b.w = {}
                b.r = {}
            b.w[key] = v
        for b in reads:
            if b.r.get(key, 0) < v:
                b.r[key] = v

    def _cost(self, e, n, f):
        if e == "pe":
            return 0.03 + max(n, 64) * f / 1800.0
        if e == "act":
            return 0.22 + n * f / 1400.0
        if e == "dve":
            return 0.15 + n * f / 960.0
        if e == "pool":
            return 0.3 + n * f / 520.0
        return 0.05

    def _ready(self, e, reads, writes):
        t = 0.0
        for b in reads:
            for k, v in b.w.items():
                t = max(t, self.tfin.get((k, v), 0.0))
            if b.excl:
                for k, v in b.r.items():
                    t = max(t, self.tfin.get((k, v), 0.0))
        for b in writes:
            for k, v in b.w.items():
                t = max(t, self.tfin.get((k, v), 0.0))
            for k, v in b.r.items():
                t = max(t, self.tfin.get((k, v), 0.0))
        return t + 0.15

    def op(self, e, fn, reads=(), writes=(), n=512, f=1.0):
        h = self.hooks.get(threading.get_ident())
        if h is not None:
            h(lambda: max(self.efree[e], self._ready(e, reads, writes)))
        self._deps(e, reads, writes)
        ins = fn(self.eng[e])
        self.val[e] += 1
        ins.then_inc(self.sems[e], 1)
        self._mark(e, self.val[e], reads, writes)
        t0 = max(self.efree[e], self._ready(e, reads, writes))
        t1 = t0 + self._cost(e, n, f)
        self.efree[e] = t1
        self.tfin[(e, self.val[e])] = t1
        self.nops += 1

    def dma(self, q, out_ap, in_ap, reads=(), writes=(), mb=0.5):
        self._deps(q, reads, writes)
        keys = self.dpool[q]
        key = keys[self.dnext[q] % len(keys)]
        self.dnext[q] += 1
        if self.val[key] > 0:
            self._wait(q, key, self.val[key])
        ins = self.eng[q].dma_start(out=out_ap, in_=in_ap)
        self.val[key] += 16
        ins.then_inc(self.sems[key], 16)
        self._mark(key, self.val[key], reads, writes)
        t0 = max(self.efree[q], self._ready(q, reads, writes))
        self.efree[q] = t0 + 0.1
        self.tfin[(key, self.val[key])] = t0 + 2.0 + mb * 4.0
        return key, self.val[key]

    def barrier(self, engines=("pe", "act", "dve", "pool")):
        for e in engines:
            for k in engines:
                if k != e and self.val[k] > 0:
                    self._wait(e, k, self.val[k])

    def run_streams(self, fns, prio=None):
        n = len(fns)
        go = [threading.Semaphore(0) for _ in range(n)]
        back = threading.Semaphore(0)
        st = {"done": [False] * n, "err": None, "est": [None] * n}

        def mk(i):
            def hook(est=None):
                st["est"][i] = est
                back.release()
                go[i].acquire()

            def runner():
                go[i].acquire()
                self.hooks[threading.get_ident()] = hook
                try:
                    fns[i](hook)
                except BaseException as ex:
                    st["err"] = ex
                finally:
                    self.hooks.pop(threading.get_ident(), None)
                    st["done"][i] = True
                    back.release()
            t = threading.Thread(target=runner, daemon=True)
            t.start()
            return t

        ths = [mk(i) for i in range(n)]
        for i in range(n):
            go[i].release()
            back.acquire()
            if st["err"] is not None:
                raise st["err"]
        lastpick = [0] * n
        tick = 0
        while not all(st["done"]):
            best, bt = None, None
            for i in range(n):
                if st["done"][i]:
                    continue
                t = st["est"][i]() if st["est"][i] is not None else 1e30
                if prio is not None and 0.0 <= t < 1e29:
                    t = max(0.0, t - prio[i])
                if not LIST_SCHED and 0.0 <= t < 1e29:
                    t = 0.0
                if best is None or t < bt or (t == bt and lastpick[i] < lastpick[best]):
                    best, bt = i, t
            tick += 1
            lastpick[best] = tick
            go[best].release()
            back.acquire()
            if st["err"] is not None:
                raise st["err"]
        for t in ths:
            t.join()


class _Stop(Exception):
    pass


def build_program(stop=None, dbg_cols=0):
    nc = bass.Bass("TRN2", target_bir_lowering=False)
    dram = nc.dram_tensor
    xT_d = dram("xT", [128, KC, T], F32, kind="ExternalInput").ap()
    par_d = dram("params", [128, NPAR], F32, kind="ExternalInput").ap()
    cst_d = dram("cst", [128, NCST], F32, kind="ExternalInput").ap()
    rope_d = dram("rope", [128, 2, SEQS], F32, kind="ExternalInput").ap()
    s0_d = dram("s0", [128, L * 2 * 8, 128], F32, kind="ExternalInput").ap()
    adaw_d = dram("ada_w", [L, 128, 48, 1024], F32, kind="ExternalInput").ap()
    win_d = dram("w_in", [L, 128, 36, 1024], F32, kind="ExternalInput").ap()
    wout_d = dram("w_out", [L, 128, 8, 1024], F32, kind="ExternalInput").ap()
    wup_d = dram("w_up", [L, 128, 44, 1024], F32, kind="ExternalInput").ap()
    wdn_d = dram("w_down", [L, 128, 16, 1408], F32, kind="ExternalInput").ap()
    win_b = dram("w_in_b", [L, 128, 36, 1024], BF16, kind="Internal").ap()
    wout_b = dram("w_out_b", [L, 128, 8, 1024], BF16, kind="Internal").ap()
    wup_b = dram("w_up_b", [L, 128, 44, 1024], BF16, kind="Internal").ap()
    wdn_b = dram("w_down_b", [L, 128, 16, 1408], BF16, kind="Internal").ap()
    bnd_d = dram("bnd", [4, 128, 8, 128], F32, kind="Internal").ap()
    adaw_b = dram("ada_w_b", [L, 128, 48, 1024], BF16, kind="Internal").ap()
    yT_d = dram("yT", [128, KC, T], F32, kind="ExternalOutput").ap()
    st_d = dram("st", [128, 2 * L * 2 * 8, 128], F32, kind="ExternalOutput").ap()
    dbg_d = dram("dbg", [128, dbg_cols], F32, kind="ExternalOutput").ap() if dbg_cols else None

    es = ExitStack()
    with es:
        S = Sched(nc, es)
        out_tickets = []

        def ckpt(name, dump=None):
            if stop != name:
                return
            if dump is not None:
                ap, bufs = dump()
                n = ap.shape[-1]
                out_tickets.append(S.dma("pool", dbg_d[:, 0:n], ap, reads=bufs))
            raise _Stop()

        def sb(name, shape, dt):
            return es.enter_context(nc.sbuf_tensor(name, shape, dt))

        def ps(name, shape, dt):
            return es.enter_context(nc.psum_tensor(name, shape, dt))

        xT = sb("xT_s", [128, KC, T], F32)
        xTb = [Buf(xT[:, :, b * TB:(b + 1) * TB]) for b in range(NBLK)]
        par = Buf(sb("par_s", [128, NPAR], F32)[:, :])
        cst_t = sb("cst_s", [128, NCST], F32)
        cst = Buf(cst_t[:, :])
        ident_bf = Buf(sb("ident_bf", [128, 128], BF16)[:, :])
        rmat_bf = Buf(sb("rmat_bf", [128, 128], BF16)[:, :])
        ones1024 = Buf(sb("ones1024", [128, 128], F32)[:, :])
        ones128 = Buf(sb("ones128", [128, 128], F32)[:, :])
        ones1024b = Buf(sb("ones1024b", [128, 128], BF16)[:, :])
        ones128b = Buf(sb("ones128b", [128, 128], BF16)[:, :])
        small_t = sb("small_s", [128, 384], F32)
        small = Buf(small_t[:, :])
        scb = Buf(sb("scb_s", [128, 16], BF16)[:, :])
        rcst_t = sb("rcst_s", [128, 8, 3, 64], F32)
        rcst = Buf(rcst_t[:, :, :, :])
        ring_t = sb("ring_s", [128, NSLOT, SLOT], BF16)
        ring = [Buf(ring_t[:, i, :]) for i in range(NSLOT)]
        hT_t = sb("hT_s", [128, KC, TB], BF16)
        hT = Buf(hT_t[:, :, :])
        oc_t = sb("ocat_s", [128, KC, TB], BF16)
        ocat = Buf(oc_t[:, :, :])
        Sc_t = sb("Sc_s", [128, 8, 128], F32)
        Sc = [Buf(Sc_t[:, i, :]) for i in range(8)]
        Sin_t = sb("Sin_s", [128, 2, 128], F32)
        Sin = [Buf(Sin_t[:, i, :]) for i in range(2)]
        sin_n = [0]
        MS_N = 32832
        MS = sb("ms_s", [128, MS_N], BF16)

        def msf(off, n):
            return MS[:, off:off + 2 * n].bitcast(F32)

        Hs = []
        for i in range(2):
            o = i * 3584
            Hs.append({"qF": Buf(msf(o, 512)), "kF": Buf(msf(o + 1024, 512)), "vTf": Buf(MS[:, o + 2048:o + 2560]),
                       "vTr": Buf(MS[:, o + 2560:o + 3072]), "sgb": Buf(MS[:, o + 3072:o + 3584])})
        Us = []
        for i in range(2):
            o = 7168 + i * 7200
            lgb = Buf(msf(o, 512))
            Us.append({"lg": lgb, "e1": lgb, "kk": Buf(msf(o + 1024, 512)), "b32": Buf(msf(o + 2048, 512)),
                       "b64": Buf(msf(o + 3072, 512)), "e2": Buf(msf(o + 4096, 512)),
                       "Qt": Buf(MS[:, o + 5120:o + 5632]), "Qh": Buf(MS[:, o + 5632:o + 6144]),
                       "Kl": Buf(MS[:, o + 6144:o + 6656]), "Kh": Buf(MS[:, o + 6656:o + 7168]),
                       "dec": Buf(msf(o + 7168, 8))})
        o = 7168 + 14400
        VK = Buf(MS[:, o:o + 1024])
        Sall_ap = msf(o + 1024, 9 * 128)
        Sall = Buf(Sall_ap)
        Sbf = Buf(MS[:, o + 3328:o + 4352])
        Sfin = Buf(msf(o + 4352, 128))
        Am = Buf(MS[:, o + 4608:o + 5120])
        Amv = Am.ap.rearrange("p (j k) -> p j k", k=128)
        o += 5120
        fa = Buf(msf(o, 512))
        fb = Buf(msf(o + 1024, 512))
        fc = Buf(msf(o + 2048, 512))
        fd = Buf(msf(o + 3072, 512))
        o += 4096
        ropeb = Buf(msf(o, 1024))
        assert o + 2048 == MS_N
        actT_ap = MS[:, 0:NFF * TB]
        actT = Buf(actT_ap)
        yst_ap = msf(0, KC * TB)
        yst = Buf(yst_ap)

        pj = [Buf(ps("pj%d" % i, [128, 512], F32)[:, :], True) for i in range(2)]
        pA_t = ps("pA", [128, 4, 128], F32)
        pA_bank = Buf(pA_t[:, :, :], True)
        pS_t = ps("pS", [128, 8, 128], F32)
        pS = Buf(pS_t[:, :, :], True)
        pOf = Buf(ps("pOf", [128, 512], F32)[:, :], True)
        pOb = Buf(ps("pOb", [128, 512], F32)[:, :], True)
        pT_t = pA_t[:, :, :].rearrange("p a b -> p (a b)").bitcast(BF16).rearrange("p (j k) -> p j k", k=128)
        pT = pA_bank
        pF_t = ps("pF", [128, 512], F32)
        pF = Buf(pF_t[:, :], True)
        pjn = [0]

        def next_pj():
            pjn[0] += 1
            return pj[pjn[0] % 2]

        conv = {}

        def conv_chunks(l):
            ch = []
            for g in range(8):
                t0, n = (5 * g, 5) if g < 4 else (20 + 4 * (g - 4), 4)
                ch.append((("in", l, g), win_b[l, :, t0:t0 + n, :], win_d[l, :, t0:t0 + n, :]))
            ch.append((("out", l, 0), wout_b[l, :, :, :], wout_d[l, :, :, :]))
            for g in range(4):
                ch.append((("up", l, g), wup_b[l, :, 11 * g:11 * g + 11, :], wup_d[l, :, 11 * g:11 * g + 11, :]))
            for g in range(8):
                ch.append((("dn", l, g), wdn_b[l, :, 2 * g:2 * g + 2, :], wdn_d[l, :, 2 * g:2 * g + 2, :]))
            return ch

        def body():
            S.dma("sp", par.ap, par_d, writes=[par])
            S.dma("sp", cst.ap, cst_d, writes=[cst])
            conv_list = {}
            for l in range(L):
                conv_list[l] = conv_chunks(l)
                for key, ob, ib in conv_list[l]:
                    conv[key] = Buf(ob)
                for comp in range(6):
                    conv[("ada", l, comp)] = Buf(adaw_b[l, :, comp * 8:comp * 8 + 8, :])

            def issue_conv(l, i0, i1):
                for key, ob, ib in conv_list[l][i0:i1]:
                    S.dma("pool", ob, ib, writes=[conv[key]])

            issue_conv(0, 0, 8)

            S.op("dve", lambda e: e.tensor_copy(out=ident_bf.ap, in_=cst_t[:, C_ID:C_ID + 128]), reads=[cst], writes=[ident_bf])
            S.op("dve", lambda e: e.tensor_copy(out=rmat_bf.ap, in_=cst_t[:, C_RM:C_RM + 128]), reads=[cst], writes=[rmat_bf])
            S.op("dve", lambda e: e.memset(ones1024.ap, 1.0 / 1024.0), writes=[ones1024])
            S.op("dve", lambda e: e.memset(ones128.ap, 1.0 / 128.0), writes=[ones128])
            S.op("dve", lambda e: e.memset(ones1024b.ap, 1.0 / 1024.0), writes=[ones1024b])
            S.op("dve", lambda e: e.memset(ones128b.ap, 1.0 / 128.0), writes=[ones128b])
            m32_ap = cst_t[:, C_M32:C_M32 + 512]
            m64_ap = cst_t[:, C_M64:C_M64 + 512]
            mask_ap = cst_t[:, C_MASK:C_MASK + 128]
            tp1_ap = cst_t[:, C_TP1:C_TP1 + 64]
            tm_ap = cst_t[:, C_TM:C_TM + 64]

            O_SC = 0
            O_MOD = 16
            O_LB = 208
            O_OML = 224
            O_NOML = 240
            O_LGAM = 256
            O_NLGAM = 272
            O_RDEC = 288
            O_A = 304
            par_t = par.ap

            def sm(c0, n=1):
                return small_t[:, c0:c0 + n]

            S.op("act", lambda e: e.activation(out=sm(O_SC, 16), in_=par_t[:, P_COND:P_COND + 16], func=AF.Silu),
                 reads=[par], writes=[small])
            S.op("dve", lambda e: e.memset(sm(O_LB, 8), 0.0), writes=[small])
            S.op("dve", lambda e: e.tensor_tensor(out=sm(O_LB + 8, 8), in0=par_t[:, P_LB + 8:P_LB + 16],
                                                  in1=par_t[:, P_LB:P_LB + 8], op=ALU.subtract), reads=[par], writes=[small])
            S.op("act", lambda e: e.activation(out=sm(O_LB + 8, 8), in_=sm(O_LB + 8, 8), func=AF.Sigmoid),
                 reads=[small], writes=[small])
            S.op("dve", lambda e: e.tensor_scalar(out=sm(O_OML, 16), in0=sm(O_LB, 16), scalar1=-1.0, scalar2=1.0,
                                                  op0=ALU.mult, op1=ALU.add), reads=[small], writes=[small])
            S.op("dve", lambda e: e.tensor_scalar(out=sm(O_NOML, 16), in0=sm(O_LB, 16), scalar1=1.0, scalar2=-1.0,
                                                  op0=ALU.mult, op1=ALU.add), reads=[small], writes=[small])
            S.op("act", lambda e: e.activation(out=sm(O_LGAM, 16), in_=par_t[:, P_RD:P_RD + 16], func=AF.Sigmoid),
                 reads=[par], writes=[small])
            S.op("act", lambda e: e.activation(out=sm(O_LGAM, 16), in_=sm(O_LGAM, 16), func=AF.Ln),
                 reads=[small], writes=[small])
            S.op("dve", lambda e: e.tensor_scalar(out=sm(O_NLGAM, 16), in0=sm(O_LGAM, 16), scalar1=-1.0, scalar2=None,
                                                  op0=ALU.mult), reads=[small], writes=[small])
            S.op("act", lambda e: e.activation(out=sm(O_RDEC, 16), in_=sm(O_LGAM, 16), func=AF.Exp, scale=64.0),
                 reads=[small], writes=[small])
            S.op("dve", lambda e: e.tensor_copy(out=scb.ap, in_=sm(O_SC, 16)), reads=[small], writes=[scb])
            ckpt("pre", lambda: (small_t[:, :], [small]))

            def modcol(l, c, comp, kc):
                o_ = O_MOD + l * 96 + (comp * 8 + kc) * 2 + c
                return small_t[:, o_:o_ + 1]

            def acol(l, c, mf, kc):
                o_ = O_A + ((l * 2 + c) * 2 + mf) * 8 + kc
                return small_t[:, o_:o_ + 1]

            wsched = []
            wstate = {"issued": 0, "used": 0}

            def issue_loads(upto):
                while wstate["issued"] < min(upto, len(wsched)):
                    i = wstate["issued"]
                    kind, src, cb, n = wsched[i]
                    slot = ring[i % NSLOT]
                    if kind == "ada32":
                        dst = slot.ap[:, 0:2048].bitcast(F32)
                        S.dma("sp", dst, src, writes=[slot])
                    elif kind == "ada":
                        assert cb.w, "adaLN tile used before its conversion was issued"
                        S.dma("sp", slot.ap[:, 0:1024], src, reads=[cb], writes=[slot])
                    else:
                        if kind == "dn":
                            dst = slot.ap[:, 0:2816].rearrange("p (t k) -> p t k", k=1408)
                        else:
                            dst = slot.ap[:, 0:n * 1024].rearrange("p (t k) -> p t k", k=1024)
                        assert cb.w, "weight group used before its conversion was issued"
                        S.dma("sp", dst, src, reads=[cb], writes=[slot])
                    wstate["issued"] += 1

            def next_w(expect):
                i = wstate["used"]
                assert wsched[i][0] == expect[0] and wsched[i][3] == expect[1], (i, wsched[i][0], wsched[i][3], expect)
                issue_loads(i + NSLOT - 1)
                wstate["used"] += 1
                return ring[i % NSLOT]

            def wtile(slot, t):
                return slot.ap[:, t * 1024:(t + 1) * 1024]

            ada_todo = [(0, j) for j in range(16, 48)] + [(1, j) for j in range(48)]
            ada_done = {0: 0, 1: 0}

            def ada_tile(l, j, fp32=False):
                slot = next_w(("ada32" if fp32 else "ada", 1))
                pm = next_pj()
                if fp32:
                    wf = slot.ap[:, 0:2048].bitcast(F32)
                    for kc in range(KC):
                        S.op("pe", lambda e: e.matmul(pm.ap[:, 0:2], lhsT=wf[:, kc * 128:(kc + 1) * 128],
                                                      rhs=sm(O_SC + 2 * kc, 2), start=(kc == 0), stop=(kc == KC - 1)),
                             reads=[slot, small], writes=[pm], n=64, f=4.0)
                else:
                    for kc in range(KC):
                        S.op("pe", lambda e: e.matmul(pm.ap[:, 0:2], lhsT=slot.ap[:, kc * 128:(kc + 1) * 128],
                                                      rhs=scb.ap[:, 2 * kc:2 * kc + 2], start=(kc == 0), stop=(kc == KC - 1)),
                             reads=[slot, scb], writes=[pm], n=64)
                o_ = O_MOD + l * 96 + 2 * j
                S.op("dve", lambda e: e.tensor_scalar(out=sm(o_, 2), in0=pm.ap[:, 0:2],
                                                      scalar1=par_t[:, P_ADAB + l * 48 + j:P_ADAB + l * 48 + j + 1],
                                                      scalar2=None, op0=ALU.add), reads=[pm, par], writes=[small])
                ada_done[l] += 1
                for mf, comp in ((0, 1), (1, 4)):
                    if j == comp * 8 + 7:
                        for c in range(2):
                            scv = small_t[:, O_MOD + l * 96 + comp * 16 + c:O_MOD + l * 96 + comp * 16 + 16:2]
                            nw0 = (P_NMIX if mf == 0 else P_NFFN) + l * 8
                            S.op("dve", lambda e: e.scalar_tensor_tensor(
                                out=sm(O_A + ((l * 2 + c) * 2 + mf) * 8, 8), in0=scv, scalar=1.0, in1=par_t[:, nw0:nw0 + 8],
                                op0=ALU.add, op1=ALU.mult), reads=[small, par], writes=[small])

            def ada_some(n):
                for _ in range(n):
                    if ada_todo:
                        l_, j_ = ada_todo.pop(0)
                        ada_tile(l_, j_)

            def xcols(b, kc):
                return xT[:, kc, b * TB:(b + 1) * TB]

            def rstd_from(pn, dst):
                S.op("act", lambda e: e.activation(out=dst.ap, in_=pn.ap, func=AF.Ln, bias=EPS, scale=1.0),
                     reads=[pn], writes=[dst])
                S.op("act", lambda e: e.activation(out=dst.ap, in_=dst.ap, func=AF.Exp, scale=-0.5),
                     reads=[dst], writes=[dst])

            def ssq_block(b):
                pn = next_pj()
                sq = [fa, fb]
                for kc in range(KC):
                    s = sq[kc % 2]
                    sbf = s.ap.bitcast(BF16)[:, 0:TB]
                    S.op("act", lambda e: e.activation(out=sbf, in_=xcols(b, kc), func=AF.Square),
                         reads=[xTb[b]], writes=[s])
                    S.op("pe", lambda e: e.matmul(pn.ap, lhsT=ones1024b.ap, rhs=sbf, start=(kc == 0), stop=(kc == KC - 1)),
                         reads=[s, ones1024b], writes=[pn])
                rstd_from(pn, fc)

            def norm_block(b, l, c, mf):
                ssq_block(b)
                tmp = [fd, fa]
                shcomp = 0 if mf == 0 else 3
                for kc in range(KC):
                    t = tmp[kc % 2]
                    S.op("dve", lambda e: e.scalar_tensor_tensor(out=t.ap, in0=xcols(b, kc), scalar=acol(l, c, mf, kc),
                                                                 in1=fc.ap, op0=ALU.mult, op1=ALU.mult),
                         reads=[xTb[b], fc, small], writes=[t])
                    S.op("act", lambda e: e.activation(out=hT_t[:, kc, :], in_=t.ap, func=AF.Identity,
                                                       bias=modcol(l, c, shcomp, kc), scale=1.0),
                         reads=[t, small], writes=[hT])

            def proj(slot, t, pb):
                w = wtile(slot, t)
                for kc in range(KC):
                    S.op("pe", lambda e: e.matmul(pb.ap, lhsT=w[:, kc * 128:(kc + 1) * 128], rhs=hT_t[:, kc, :],
                                                  start=(kc == 0), stop=(kc == KC - 1)),
                         reads=[slot, hT], writes=[pb])

            def rev(ap):
                return ap[:, ::-1]

            def gate_evac(pb, l, d, h, reverse, U):
                o_ = (l * 2 + d) * 4 + h
                src = rev(pb.ap) if reverse else pb.ap
                e2, lg, kk = U["e2"], U["lg"], U["kk"]
                S.op("act", lambda e: e.activation(out=e2.ap, in_=src, func=AF.Exp, scale=-1.0), reads=[pb], writes=[e2])
                S.op("act", lambda e: e.activation(out=e2.ap, in_=e2.ap, func=AF.Ln, bias=1.0, scale=1.0), reads=[e2], writes=[e2])
                S.op("act", lambda e: e.activation(out=e2.ap, in_=e2.ap, func=AF.Exp, scale=-1.0), reads=[e2], writes=[e2])
                S.op("act", lambda e: e.activation(out=lg.ap, in_=e2.ap, func=AF.Ln, scale=sm(O_OML + o_), bias=sm(O_LB + o_)),
                     reads=[e2, small], writes=[lg])
                S.op("pool", lambda e: e.tensor_scalar(out=kk.ap, in0=e2.ap, scalar1=sm(O_NOML + o_), scalar2=sm(O_OML + o_),
                                                       op0=ALU.mult, op1=ALU.add), reads=[e2, small], writes=[kk])
                S.op("pool", lambda e: e.tensor_scalar(out=lg.ap, in0=lg.ap, scalar1=0.0, scalar2=LOG_FLOOR, op0=ALU.min, op1=ALU.max),
                     reads=[lg], writes=[lg])

            kh_ready = [0]

            def c3(ap, n=64):
                return ap.rearrange("p (c t) -> p c t", t=n)

            def hgrn_prep(qF, reverse, states_only, U, part=0):
                lg, kk, b32, b64, e1, e2 = U["lg"], U["kk"], U["b32"], U["b64"], U["e1"], U["e2"]
                Qt, Qh, Kl, Kh, decb = U["Qt"], U["Qh"], U["Kl"], U["Kh"], U["dec"]
                if part in (0, 1):
                    hgrn_prep1(U)
                if states_only or part == 1:
                    return
                hgrn_prep2(qF, reverse, U)

            def hgrn_prep1(U):
                lg, kk, b32, b64, e1, e2 = U["lg"], U["kk"], U["b32"], U["b64"], U["e1"], U["e2"]
                Qt, Qh, Kl, Kh, decb = U["Qt"], U["Qh"], U["Kl"], U["Kh"], U["dec"]
                S.op("dve", lambda e: e.tensor_tensor_scan(out=b64.ap, data0=m64_ap, data1=lg.ap, initial=0.0,
                                                           op0=ALU.mult, op1=ALU.add), reads=[cst, lg], writes=[b64], f=2.2)
                tot64 = c3(b64.ap)[:, :, 63]
                S.op("act", lambda e: e.activation(out=decb.ap, in_=tot64, func=AF.Exp), reads=[b64], writes=[decb], n=8)
                S.op("dve", lambda e: e.tensor_tensor(out=c3(e2.ap), in0=tot64.unsqueeze(2).to_broadcast([128, 8, 64]),
                                                      in1=c3(b64.ap), op=ALU.subtract), reads=[b64], writes=[e2])
                S.op("act", lambda e: e.activation(out=e2.ap, in_=e2.ap, func=AF.Exp), reads=[e2], writes=[e2])
                S.op("dve", lambda e: e.tensor_tensor(out=Kh.ap, in0=kk.ap, in1=e2.ap, op=ALU.mult),
                     reads=[kk, e2], writes=[Kh])
                kh_ready[0] += 1

            def hgrn_prep2(qF, reverse, U):
                lg, kk, b32, b64, e1, e2 = U["lg"], U["kk"], U["b32"], U["b64"], U["e1"], U["e2"]
                Qt, Qh, Kl, Kh, decb = U["Qt"], U["Qh"], U["Kl"], U["Kh"], U["dec"]
                qsrc_ap = rev(qF.ap) if reverse else qF.ap
                S.op("dve", lambda e: e.tensor_tensor_scan(out=b32.ap, data0=m32_ap, data1=lg.ap, initial=0.0,
                                                           op0=ALU.mult, op1=ALU.add), reads=[cst, lg], writes=[b32], f=2.2)
                S.op("act", lambda e: e.activation(out=e1.ap, in_=b32.ap, func=AF.Exp), reads=[b32], writes=[e1])
                S.op("pool", lambda e: e.tensor_tensor(out=Qt.ap, in0=qsrc_ap, in1=e1.ap, op=ALU.mult),
                     reads=[qF, e1], writes=[Qt])
                S.op("act", lambda e: e.activation(out=e2.ap, in_=b64.ap, func=AF.Exp), reads=[b64], writes=[e2])
                S.op("pool", lambda e: e.tensor_tensor(out=Qh.ap, in0=qsrc_ap, in1=e2.ap, op=ALU.mult),
                     reads=[qF, e2], writes=[Qh])
                S.op("act", lambda e: e.activation(out=e1.ap, in_=b32.ap, func=AF.Exp, scale=-1.0), reads=[b32], writes=[e1])
                S.op("pool", lambda e: e.tensor_tensor(out=Kl.ap, in0=kk.ap, in1=e1.ap, op=ALU.mult),
                     reads=[kk, e1], writes=[Kl])

            def ret_prep(Hd, reverse, l, d, h, states_only, U, part=0):
                i = d * 4 + h
                qF, kF = Hd["qF"], Hd["kF"]
                ksrc_ap = rev(kF.ap) if reverse else kF.ap
                if part in (0, 1):
                    S.op("pool", lambda e: e.tensor_tensor(out=c3(U["Kh"].ap), in0=c3(ksrc_ap),
                                                          in1=rcst_t[:, i, 2, :].unsqueeze(1).to_broadcast([128, 8, 64]),
                                                          op=ALU.mult), reads=[kF, rcst], writes=[U["Kh"]])
                    kh_ready[0] += 1
                if states_only or part == 1:
                    return
                qsrc_ap = rev(qF.ap) if reverse else qF.ap
                S.op("pool", lambda e: e.tensor_tensor(out=c3(U["Qt"].ap), in0=c3(qsrc_ap),
                                                      in1=rcst_t[:, i, 0, :].unsqueeze(1).to_broadcast([128, 8, 64]),
                                                      op=ALU.mult), reads=[qF, rcst], writes=[U["Qt"]])
                S.op("pool", lambda e: e.tensor_tensor(out=c3(U["Kl"].ap), in0=c3(ksrc_ap),
                                                      in1=rcst_t[:, i, 1, :].unsqueeze(1).to_broadcast([128, 8, 64]),
                                                      op=ALU.mult), reads=[kF, rcst], writes=[U["Kl"]])

            VKv = VK.ap.rearrange("p (j k) -> p j k", k=128)
            Sallv = Sall_ap.rearrange("p (c k) -> p c k", k=128)
            Sbfv = Sbf.ap.rearrange("p (c k) -> p c k", k=128)

            def scan_core(vT, U, Qh_b, dec_fn, init, two_level, pO, states_only, prompt, fin_fn, gate_full=None):
                Qt, Kl, Kh, decb = U["Qt"], U["Kl"], U["Kh"], U["dec"]
                for j in range(4):
                    S.op("pe", lambda e: e.transpose(pT_t[:, j, :], vT.ap[:, j * 128:(j + 1) * 128], ident_bf.ap),
                         reads=[vT, ident_bf], writes=[pT], n=128)
                for j in range(4):
                    S.op("pe", lambda e: e.transpose(pT_t[:, 4 + j, :], Kh.ap[:, j * 128:(j + 1) * 128], ident_bf.ap),
                         reads=[Kh, ident_bf], writes=[pT], n=128)
                S.op("act", lambda e: e.activation(out=VKv, in_=pT_t[:, :, :], func=AF.Copy), reads=[pT], writes=[VK], n=1024)
                for c in range(8):
                    j, po = c // 2, 64 * (c % 2)
                    S.op("pe", lambda e: e.matmul(pS_t[:, (c % 2) * 4 + j, :], lhsT=VKv[po:po + 64, 4 + j, :],
                                                  rhs=VKv[po:po + 64, j, :], start=True, stop=True), reads=[VK], writes=[pS], n=128)
                if init[0] == "buf":
                    S.op("dve", lambda e: e.tensor_copy(out=Sallv[:, 0, :], in_=init[1].ap), reads=[init[1]], writes=[Sall], n=128)
                elif init[0] == "zero":
                    S.op("dve", lambda e: e.memset(Sallv[:, 0, :], 0.0), writes=[Sall], n=128)
                else:
                    sin_n[0] += 1
                    sb_ = Sin[sin_n[0] % 2]
                    S.dma("sp", sb_.ap, init[1], reads=([init[2]] if init[2] is not None else []), writes=[sb_])
                    S.op("dve", lambda e: e.tensor_copy(out=Sallv[:, 0, :], in_=sb_.ap), reads=[sb_], writes=[Sall], n=128)
                if prompt:
                    S.op("dve", lambda e: e.memset(Sallv[:, 4, :], 0.0), writes=[Sall], n=128)
                for c in range(8):
                    if prompt and c == 3:
                        dst_ap, dst_b = Sfin.ap, Sfin
                    else:
                        dst_ap, dst_b = Sallv[:, c + 1, :], Sall
                    S.op("dve", lambda e: e.scalar_tensor_tensor(out=dst_ap, in0=Sallv[:, c, :], scalar=dec_fn(c),
                                                                 in1=pS_t[:, (c % 2) * 4 + c // 2, :], op0=ALU.mult, op1=ALU.add),
                         reads=[Sall, pS, decb, small], writes=[dst_b], n=128)
                if prompt:
                    fin_fn(0, Sfin, Sfin.ap)
                fin_fn(1, Sall, Sallv[:, 8, :])
                if states_only:
                    return
                S.op("act", lambda e: e.activation(out=Sbf.ap, in_=Sall_ap[:, 0:1024], func=AF.Copy), reads=[Sall], writes=[Sbf], n=1024)
                if gate_full is not None:
                    gate_full()
                if not states_only:
                    for j in range(4):
                        cs = slice(j * 128, (j + 1) * 128)
                        S.op("pe", lambda e: e.matmul(pA_t[:, j, :], lhsT=Kl.ap[:, cs], rhs=Qt.ap[:, cs], start=True, stop=True),
                             reads=[Kl, Qt], writes=[pA_bank], n=128)
                        if two_level:
                            for hh in range(2):
                                o_ = j * 128 + hh * 64
                                S.op("pe", lambda e: e.matmul(pA_t[hh * 64:hh * 64 + 32, j, hh * 64 + 32:hh * 64 + 64],
                                                              lhsT=Kl.ap[:, o_:o_ + 32], rhs=Qh_b.ap[:, o_ + 32:o_ + 64],
                                                              start=True, stop=True), reads=[Kl, Qh_b], writes=[pA_bank], n=32)
                    S.op("dve", lambda e: e.tensor_tensor(out=Amv, in0=pA_t[:, :, :],
                                                          in1=mask_ap.unsqueeze(1).to_broadcast([128, 4, 128]), op=ALU.mult),
                         reads=[pA_bank, cst], writes=[Am])

                    for j in range(4):
                        cs = slice(j * 128, (j + 1) * 128)
                        S.op("pe", lambda e: e.matmul(pO.ap[:, cs], lhsT=VKv[:, j, :], rhs=Amv[:, j, :], start=(j == 0), stop=False),
                             reads=[VK, Am], writes=[pO], n=128)
                for c in range(8):
                    S.op("pe", lambda e: e.matmul(pO.ap[:, c * 64:(c + 1) * 64], lhsT=Sbfv[:, c, :],
                                                  rhs=Qh_b.ap[:, c * 64:(c + 1) * 64], start=False, stop=(c == 7)),
                         reads=[Sbf, Qh_b], writes=[pO], n=64)

            def finalize_evac():
                S.op("act", lambda e: e.activation(out=fa.ap, in_=rev(pOb.ap), func=AF.Copy), reads=[pOb], writes=[fa])
                S.op("dve", lambda e: e.tensor_tensor(out=fb.ap, in0=pOf.ap, in1=fa.ap, op=ALU.add),
                     reads=[pOf, fa], writes=[fb])

            def finalize_rest(l, mixer, h, Hd):
                center = (mixer == 1)
                sgb = Hd["sgb"]
                if center:
                    S.op("pe", lambda e: e.matmul(pF.ap, lhsT=ones128.ap, rhs=fb.ap, start=True, stop=True),
                         reads=[ones128, fb], writes=[pF], f=4.0)
                    S.op("dve", lambda e: e.tensor_tensor(out=fb.ap, in0=fb.ap, in1=pF.ap, op=ALU.subtract),
                         reads=[fb, pF], writes=[fb])
                fabf = fa.ap.bitcast(BF16)[:, 0:TB]
                S.op("act", lambda e: e.activation(out=fabf, in_=fb.ap, func=AF.Square), reads=[fb], writes=[fa])
                S.op("pe", lambda e: e.matmul(pF.ap, lhsT=ones128b.ap, rhs=fabf, start=True, stop=True),
                     reads=[ones128b, fa], writes=[pF])
                S.op("act", lambda e: e.activation(out=fc.ap, in_=pF.ap, func=AF.Ln, bias=EPS, scale=1.0),
                     reads=[pF], writes=[fc])
                S.op("act", lambda e: e.activation(out=fc.ap, in_=fc.ap, func=AF.Exp, scale=-0.5),
                     reads=[fc], writes=[fc])
                S.op("pool", lambda e: e.tensor_tensor(out=fd.ap, in0=fb.ap, in1=fc.ap, op=ALU.mult),
                     reads=[fb, fc], writes=[fd])
                g0 = (P_HN if mixer == 0 else P_RN) + l * 4 + h
                S.op("dve", lambda e: e.scalar_tensor_tensor(out=oc_t[:, mixer * 4 + h, :], in0=fd.ap, scalar=par_t[:, g0:g0 + 1],
                                                             in1=sgb.ap, op0=ALU.mult, op1=ALU.mult),
                     reads=[fd, par, sgb], writes=[ocat])

            def rope_evac(pb, dst, do_rope, U):
                if not do_rope:
                    S.op("act", lambda e: e.activation(out=dst.ap, in_=pb.ap, func=AF.Copy), reads=[pb], writes=[dst])
                    return
                Qt, e1, e2 = U["Qt"], U["e1"], U["e2"]
                S.op("act", lambda e: e.activation(out=Qt.ap, in_=pb.ap, func=AF.Copy), reads=[pb], writes=[Qt])
                pr = next_pj()
                S.op("pe", lambda e: e.matmul(pr.ap, lhsT=rmat_bf.ap, rhs=Qt.ap, start=True, stop=True),
                     reads=[rmat_bf, Qt], writes=[pr])
                rp = ropeb.ap.rearrange("p (a t) -> p a t", a=2)
                S.op("dve", lambda e: e.tensor_tensor(out=e1.ap, in0=pb.ap, in1=rp[:, 0, :], op=ALU.mult),
                     reads=[pb, ropeb], writes=[e1])
                S.op("dve", lambda e: e.tensor_tensor(out=e2.ap, in0=pr.ap, in1=rp[:, 1, :], op=ALU.mult),
                     reads=[pr, ropeb], writes=[e2])
                S.op("pool", lambda e: e.tensor_tensor(out=dst.ap, in0=e1.ap, in1=e2.ap, op=ALU.add),
                     reads=[e1, e2], writes=[dst])

            def v_evac(pb, want_fwd, Hd):
                if want_fwd:
                    S.op("act", lambda e: e.activation(out=Hd["vTf"].ap, in_=pb.ap, func=AF.Copy), reads=[pb], writes=[Hd["vTf"]])
                S.op("act", lambda e: e.activation(out=Hd["vTr"].ap, in_=rev(pb.ap), func=AF.Copy), reads=[pb], writes=[Hd["vTr"]])

            def load_rope(b):
                p0 = (b - 1) * TB
                S.dma("sp", ropeb.ap.rearrange("p (a t) -> p a t", a=2), rope_d[:, :, p0:p0 + TB], writes=[ropeb])

            def ret_consts(l):
                for d in range(2):
                    for h in range(4):
                        i = d * 4 + h
                        o_ = (l * 2 + d) * 4 + h
                        S.op("act", lambda e: e.activation(out=rcst_t[:, i, 0, :], in_=tp1_ap, func=AF.Exp, scale=sm(O_LGAM + o_)),
                             reads=[cst, small], writes=[rcst])
                        S.op("act", lambda e: e.activation(out=rcst_t[:, i, 1, :], in_=tp1_ap, func=AF.Exp, scale=sm(O_NLGAM + o_),
                                                           bias=LN_KSCALE), reads=[cst, small], writes=[rcst])
                        S.op("act", lambda e: e.activation(out=rcst_t[:, i, 2, :], in_=tm_ap, func=AF.Exp, scale=sm(O_LGAM + o_),
                                                           bias=LN_KSCALE), reads=[cst, small], writes=[rcst])

            def st_index(j, l, d, mh):
                return ((j * L + l) * 2 + d) * 8 + mh

            bndb = [[Buf(bnd_d[b, :, mh, :]) for mh in range(8)] for b in range(4)]

            def run_units(units, heads=None):
                prog = {"a": 0, "b": 0, "c": 0}
                n = len(units)

                def gate(hook, cond):
                    while cond():
                        hook(lambda: 1e30 if cond() else -1.0)

                def stream_a(hook):
                    for k in range(n):
                        gate(hook, lambda: prog["b"] < k - 1)
                        if heads is not None and k % 2 == 0:
                            gate(hook, lambda: prog["c"] < k // 2 - 1)
                        units[k][0](k)
                        prog["a"] = k + 1

                kh0 = kh_ready[0]

                def stream_b(hook):
                    for k in range(n):
                        gate(hook, lambda: kh_ready[0] - kh0 < k + 1 and prog["a"] < k + 1)
                        if heads is not None and k % 2 == 1:
                            gate(hook, lambda: prog["c"] < k // 2)
                        units[k][1](k, lambda: gate(hook, lambda: prog["a"] < k + 1))
                        prog["b"] = k + 1

                def stream_c(hook):
                    for h in range(len(heads)):
                        gate(hook, lambda: prog["b"] < 2 * h + 2)
                        heads[h](h)
                        prog["c"] = h + 1

                S.run_streams([stream_a, stream_b] + ([stream_c] if heads is not None else []),
                              prio=[0.5, 0.0] + ([0.0] if heads is not None else []))

            def pass1(l):
                for mh in range(8):
                    S.dma("sp", Sc[mh].ap, s0_d[:, (l * 2 + 1) * 8 + mh, :], writes=[Sc[mh]])
                for b in (4, 3, 2):
                    norm_block(b, l, 1, 0)
                    load_rope(b)
                    units = []
                    for mh in range(8):
                        mixer, h = mh // 4, mh % 4

                        def a_fn(k, mixer=mixer, h=h):
                            U, Hd = Us[k % 2], Hs[k % 2]
                            slot = next_w(("in", 2))
                            pz = next_pj()
                            proj(slot, 0, pz)
                            if mixer == 0:
                                gate_evac(pz, l, 1, h, True, U)
                            else:
                                rope_evac(pz, Hd["kF"], True, U)
                            pv = next_pj()
                            proj(slot, 1, pv)
                            v_evac(pv, False, Hd)
                            if mixer == 0:
                                hgrn_prep(None, True, True, U)
                            else:
                                ret_prep(Hd, True, l, 1, h, True, U)
                            ada_some(2)

                        def b_fn(k, gate_full, mixer=mixer, h=h, mh=mh, b=b):
                            U, Hd = Us[k % 2], Hs[k % 2]
                            if mixer == 0:
                                dec_fn = (lambda c: U["dec"].ap[:, c:c + 1])
                            else:
                                dec_fn = (lambda c, o_=(l * 2 + 1) * 4 + h: sm(O_RDEC + o_))

                            def fin(which, sbuf, sap):
                                S.op("dve", lambda e: e.tensor_copy(out=Sc[mh].ap, in_=sap), reads=[sbuf], writes=[Sc[mh]], n=128)
                                S.dma("sp", bnd_d[b - 1, :, mh, :], sap, reads=[sbuf], writes=[bndb[b - 1][mh]])
                            scan_core(Hd["vTr"], U, None, dec_fn, ("buf", Sc[mh]), False, None, True, False, fin)
                        units.append((a_fn, b_fn))
                    run_units(units)
                    if l == 0 and not deferred["l0_rest"]:
                        deferred["l0_rest"] = True
                        for comp in range(6):
                            S.dma("pool", adaw_b[1, :, comp * 8:comp * 8 + 8, :], adaw_d[1, :, comp * 8:comp * 8 + 8, :],
                                  writes=[conv[("ada", 1, comp)]])
                        issue_conv(0, 8, len(conv_list[0]))

            def pass2(l, last):
                for mh in range(8):
                    S.dma("sp", Sc[mh].ap, s0_d[:, (l * 2 + 0) * 8 + mh, :], writes=[Sc[mh]])
                for b in range(NBLK):
                    prompt = (b == 0)
                    c = 0 if prompt else 1
                    norm_block(b, l, c, 0)
                    if not prompt:
                        load_rope(b)
                    units = []
                    hold = {}
                    for u in range(16):
                        mh, d = u // 2, u % 2
                        mixer, h = mh // 4, mh % 4

                        def a_fn(k, mixer=mixer, h=h, mh=mh, d=d):
                            U, Hd = Us[k % 2], Hs[mh % 2]
                            if d == 0:
                                ada_some(4)
                                s1 = next_w(("in", 3))
                                s2 = next_w(("in", 2 if mixer == 0 else 1))
                                hold["s1"] = s1
                                if mixer == 0:
                                    pz = next_pj()
                                    proj(s1, 1, pz)
                                    gate_evac(pz, l, 0, h, False, U)
                                    pv = next_pj()
                                    proj(s2, 0, pv)
                                    v_evac(pv, True, Hd)
                                    hgrn_prep(None, False, False, U, part=1)
                                    pq = next_pj()
                                    proj(s1, 0, pq)
                                    S.op("act", lambda e: e.activation(out=Hd["qF"].ap, in_=pq.ap, func=AF.Silu),
                                         reads=[pq], writes=[Hd["qF"]])
                                    pg = next_pj()
                                    proj(s2, 1, pg)
                                    S.op("act", lambda e: e.activation(out=Hd["sgb"].ap, in_=pg.ap, func=AF.Silu),
                                         reads=[pg], writes=[Hd["sgb"]])
                                    hgrn_prep2(Hd["qF"], False, U)
                                    return
                                pv = next_pj()
                                proj(s1, 2, pv)
                                v_evac(pv, True, Hd)
                                pk = next_pj()
                                proj(s1, 1, pk)
                                rope_evac(pk, Hd["kF"], not prompt, U)
                                ret_prep(Hd, False, l, 0, h, False, U, part=1)
                                pq = next_pj()
                                proj(s1, 0, pq)
                                rope_evac(pq, Hd["qF"], not prompt, U)
                                pg = next_pj()
                                proj(s2, 0, pg)
                                S.op("act", lambda e: e.activation(out=Hd["sgb"].ap, in_=pg.ap, func=AF.Silu),
                                     reads=[pg], writes=[Hd["sgb"]])
                                ret_prep(Hd, False, l, 0, h, False, U, part=2)
                                return
                            if mixer == 0:
                                pz = next_pj()
                                proj(hold["s1"], 1 + d, pz)
                                gate_evac(pz, l, d, h, d == 1, U)
                                hgrn_prep(Hd["qF"], d == 1, False, U)
                            else:
                                ret_prep(Hd, d == 1, l, d, h, False, U)

                        def b_fn(k, gate_full, mixer=mixer, h=h, mh=mh, d=d, b=b, prompt=prompt):
                            U, Hd = Us[k % 2], Hs[mh % 2]
                            if mixer == 0:
                                dec_fn = (lambda cc: U["dec"].ap[:, cc:cc + 1])
                                Qh_b = U["Qh"]
                            else:
                                dec_fn = (lambda cc, o_=(l * 2 + d) * 4 + h: sm(O_RDEC + o_))
                                Qh_b = U["Qt"]
                            if prompt:
                                init = ("zero",)
                            elif d == 0:
                                init = ("buf", Sc[mh])
                            elif b == NBLK - 1:
                                init = ("dram", s0_d[:, (l * 2 + 1) * 8 + mh, :], None)
                            else:
                                init = ("dram", bnd_d[b, :, mh, :], bndb[b][mh])

                            def fin(which, sbuf, sap):
                                if prompt:
                                    jj = which if d == 0 else 1 - which
                                    out_tickets.append(S.dma("sp", st_d[:, st_index(jj, l, d, mh), :], sap, reads=[sbuf]))
                                elif d == 0 and b < NBLK - 1:
                                    S.op("dve", lambda e: e.tensor_copy(out=Sc[mh].ap, in_=sap), reads=[sbuf], writes=[Sc[mh]], n=128)
                            scan_core(Hd["vTf"] if d == 0 else Hd["vTr"], U, Qh_b, dec_fn, init, mixer == 0,
                                      pOf if d == 0 else pOb, False, prompt, fin, gate_full)
                            if d == 1:
                                finalize_evac()
                        units.append((a_fn, b_fn))
                    heads = []
                    for mh in range(8):
                        def c_fn(hh_, mh=mh):
                            finalize_rest(l, mh // 4, mh % 4, Hs[mh % 2])
                            ckpt("p2h%d_b%d" % (mh, b), lambda: (oc_t[:, mh, :], [ocat]))
                        heads.append(c_fn)
                    run_units(units, heads)
                    for f in range(KC):
                        if f % 3 == 0:
                            slot = next_w(("out", 3 if f < 6 else 2))
                        t = f % 3
                        po = next_pj()
                        w = wtile(slot, t)
                        for kc in range(KC):
                            S.op("pe", lambda e: e.matmul(po.ap, lhsT=w[:, kc * 128:(kc + 1) * 128], rhs=oc_t[:, kc, :],
                                                          start=(kc == 0), stop=(kc == KC - 1)), reads=[slot, ocat], writes=[po])
                        S.op("dve", lambda e: e.scalar_tensor_tensor(out=xcols(b, f), in0=po.ap, scalar=modcol(l, c, 2, f),
                                                                     in1=xcols(b, f), op0=ALU.mult, op1=ALU.add),
                             reads=[po, small, xTb[b]], writes=[xTb[b]])
                    ckpt("p2mix_b%d" % b, lambda: (xT[:, 0, b * TB:(b + 1) * TB], [xTb[b]]))
                    norm_block(b, l, c, 1)
                    S.barrier()
                    actv = actT_ap.rearrange("p (j t) -> p j t", t=TB)
                    for j in range(NFF):
                        slot = next_w(("up", 2))
                        pgate = next_pj()
                        proj(slot, 0, pgate)
                        sgt = (fa, fb)[j % 2]
                        S.op("act", lambda e: e.activation(out=sgt.ap, in_=pgate.ap, func=AF.Silu), reads=[pgate], writes=[sgt])
                        pup = next_pj()
                        proj(slot, 1, pup)
                        S.op("dve", lambda e: e.tensor_tensor(out=actv[:, j, :], in0=pup.ap, in1=sgt.ap, op=ALU.mult),
                             reads=[pup, sgt], writes=[actT])
                    for f in range(KC):
                        slot = next_w(("dn", -1))
                        po = next_pj()
                        for kc in range(NFF):
                            S.op("pe", lambda e: e.matmul(po.ap, lhsT=slot.ap[:, kc * 128:(kc + 1) * 128], rhs=actv[:, kc, :],
                                                          start=(kc == 0), stop=(kc == NFF - 1)), reads=[slot, actT], writes=[po])
                        S.op("dve", lambda e: e.scalar_tensor_tensor(out=xcols(b, f), in0=po.ap, scalar=modcol(l, c, 5, f),
                                                                     in1=xcols(b, f), op0=ALU.mult, op1=ALU.add),
                             reads=[po, small, xTb[b]], writes=[xTb[b]])
                    S.barrier()
                    ckpt("p2ffn_b%d" % b, lambda: (xT[:, 0, b * TB:(b + 1) * TB], [xTb[b]]))
                    if l + 1 < L:
                        n1 = len(conv_list[l + 1])
                        issue_conv(l + 1, (b * n1) // 3 if b < 3 else n1, ((b + 1) * n1) // 3 if b < 3 else n1)
                    if last:
                        ssq_block(b)
                        ystv = yst_ap.rearrange("p (k t) -> p k t", t=TB)
                        for kc in range(KC):
                            S.op("dve", lambda e: e.scalar_tensor_tensor(out=ystv[:, kc, :], in0=xcols(b, kc),
                                                                         scalar=par_t[:, P_NFIN + kc:P_NFIN + kc + 1], in1=fc.ap,
                                                                         op0=ALU.mult, op1=ALU.mult),
                                 reads=[xTb[b], par, fc], writes=[yst])
                        tk = S.dma("sp", yT_d[:, :, b * TB:(b + 1) * TB], ystv, reads=[yst])
                        out_tickets.append(tk)
                        for e_ in ("pe", "act", "dve"):
                            S._wait(e_, tk[0], tk[1])

            def plan():
                ada_left = [len(ada_todo)]

                def ada_plan(n):
                    for _ in range(n):
                        if ada_left[0] > 0:
                            ada_left[0] -= 1
                            wsched.append(("ada", None, None, 1))
                for l in range(L):
                    for b in (4, 3, 2):
                        for h in range(4):
                            wsched.append(("in", win_b[l, :, 5 * h + 2:5 * h + 4, :], conv[("in", l, h)], 2))
                            ada_plan(2)
                        for h in range(4):
                            wsched.append(("in", win_b[l, :, 20 + 4 * h + 1:20 + 4 * h + 3, :], conv[("in", l, 4 + h)], 2))
                            ada_plan(2)
                    for b in range(NBLK):
                        for h in range(4):
                            ada_plan(4)
                            wsched.append(("in", win_b[l, :, 5 * h:5 * h + 3, :], conv[("in", l, h)], 3))
                            wsched.append(("in", win_b[l, :, 5 * h + 3:5 * h + 5, :], conv[("in", l, h)], 2))
                        for h in range(4):
                            ada_plan(4)
                            wsched.append(("in", win_b[l, :, 20 + 4 * h:20 + 4 * h + 3, :], conv[("in", l, 4 + h)], 3))
                            wsched.append(("in", win_b[l, :, 20 + 4 * h + 3:20 + 4 * h + 4, :], conv[("in", l, 4 + h)], 1))
                        for g in range(3):
                            t0, n = 3 * g, (3 if g < 2 else 2)
                            wsched.append(("out", wout_b[l, :, t0:t0 + n, :], conv[("out", l, 0)], n))
                        for g in range(22):
                            wsched.append(("up", wup_b[l, :, 2 * g:2 * g + 2, :], conv[("up", l, (2 * g) // 11)], 2))
                        for g in range(8):
                            wsched.append(("dn", wdn_b[l, :, 2 * g:2 * g + 2, :], conv[("dn", l, g)], -1))
                it = iter(list(ada_todo))
                for i, wsc in enumerate(wsched):
                    if wsc[0] == "ada" and wsc[1] is None:
                        l_, j_ = next(it)
                        wsched[i] = ("ada", adaw_b[l_, :, j_, :], conv[("ada", l_, j_ // 8)], 1)

            for j in range(16):
                wsched.append(("ada32", adaw_d[0, :, j, :], None, 1))
            plan()
            S.dma("sp", xTb[4].ap, xT_d[:, :, 4 * TB:5 * TB], writes=[xTb[4]])
            for comp in range(2, 6):
                S.dma("pool", adaw_b[0, :, comp * 8:comp * 8 + 8, :], adaw_d[0, :, comp * 8:comp * 8 + 8, :],
                      writes=[conv[("ada", 0, comp)]])
            for j in range(16):
                ada_tile(0, j, fp32=True)
            for b in (3, 2, 0, 1):
                S.dma("sp", xTb[b].ap, xT_d[:, :, b * TB:(b + 1) * TB], writes=[xTb[b]])
            deferred = {"l0_rest": False}
            ckpt("ada", lambda: (small_t[:, :], [small]))
            for l in range(L):
                ret_consts(l)
                pass1(l)
                ckpt("p1_%d" % l, lambda: (Sc_t[:, :, :].rearrange("p a b -> p (a b)"), Sc))
                pass2(l, l == L - 1)
                ckpt("l%d" % l, lambda: (xT[:, 0, :], xTb))
            assert wstate["used"] == len(wsched), (wstate, len(wsched))
            assert not ada_todo

        try:
            body()
        except _Stop:
            pass
        last = {}
        for k, v in out_tickets:
            last[k] = max(last.get(k, 0), v)
        for k, v in last.items():
            S._wait("sp", k, v)
        for q in S.dpool:
            for k in S.dpool[q]:
                if S.val[k] > 0:
                    S._wait("sp", k, S.val[k])
        print("bass ops:", S.nops)
    return nc


def _tile_w(W, col_starts):
    K = W.shape[0]
    Wr = W.reshape(K // 128, 128, W.shape[1])
    out = np.empty((128, len(col_starts), K // 128, 128), np.float32)
    for i, c0 in enumerate(col_starts):
        out[:, i] = Wr[:, :, c0:c0 + 128].transpose(1, 0, 2)
    return out


def _consts():
    cst = np.zeros((128, NCST), np.float32)
    t = np.arange(512)
    cst[:, C_M32:C_M32 + 512] = (t % 32 != 0).astype(np.float32)[None, :]
    cst[:, C_M64:C_M64 + 512] = (t % 64 != 0).astype(np.float32)[None, :]
    s = np.arange(128)[:, None]
    tt = np.arange(128)[None, :]
    cst[:, C_MASK:C_MASK + 128] = ((s // 64 == tt // 64) & (s <= tt)).astype(np.float32)
    cst[:, C_ID:C_ID + 128] = np.eye(128, dtype=np.float32)
    R = np.zeros((128, 128), np.float32)
    for base in (0, 64):
        for i in range(32):
            R[base + i, base + i + 32] = -1.0
            R[base + 32 + i, base + i] = 1.0
    cst[:, C_RM:C_RM + 128] = R.T
    cst[:, C_TP1:C_TP1 + 64] = (np.arange(64) + 1).astype(np.float32)[None, :]
    cst[:, C_TM:C_TM + 64] = (63 - np.arange(64)).astype(np.float32)[None, :]
    pos = np.arange(SEQS)
    rows = (pos // 64).astype(np.float32)
    cols = (pos % 64).astype(np.float32)
    inv = (10000.0 ** (-np.arange(32, dtype=np.float32) / 32)).astype(np.float32)
    rope = np.zeros((128, 2, SEQS), np.float32)
    for dd in range(128):
        p = rows if dd < 64 else cols
        ang = (p * inv[dd % 32]).astype(np.float32)
        rope[dd, 0] = np.cos(ang)
        rope[dd, 1] = np.sin(ang)
    return cst, rope


_CACHE = {}


def kernel(x_prompt, x_sample, state_hgrn, state_ret, c, c_ctx, ada_w, ada_b, norm_mix, norm_ffn, w_in,
           hgrn_lb_logits, hgrn_norm, ret_decay_logit, ret_norm, w_out, w_up, w_down, norm_final, _prepare_only=False):
    f32 = lambda a: np.ascontiguousarray(np.asarray(a), dtype=np.float32)
    x_prompt, x_sample, state_hgrn, state_ret = f32(x_prompt), f32(x_sample), f32(state_hgrn), f32(state_ret)
    c, c_ctx, ada_w, ada_b = f32(c), f32(c_ctx), f32(ada_w), f32(ada_b)
    norm_mix, norm_ffn, w_in, hgrn_lb_logits = f32(norm_mix), f32(norm_ffn), f32(w_in), f32(hgrn_lb_logits)
    hgrn_norm, ret_decay_logit, ret_norm = f32(hgrn_norm), f32(ret_decay_logit), f32(ret_norm)
    w_out, w_up, w_down, norm_final = f32(w_out), f32(w_up), f32(w_down), f32(norm_final)

    if "cst" not in _CACHE:
        _CACHE["cst"] = _consts()
    cst, rope = _CACHE["cst"]

    in_cols = []
    for h in range(4):
        in_cols += [0 + 128 * h, 512 + 128 * h, 1024 + 128 * h, 1536 + 128 * h, 2048 + 128 * h]
    for h in range(4):
        in_cols += [2560 + 128 * h, 3072 + 128 * h, 3584 + 128 * h, 4096 + 128 * h]
    up_cols = []
    for j in range(NFF):
        up_cols += [128 * j, DFF + 128 * j]
    adaw_t = np.stack([_tile_w(ada_w[l], [128 * j for j in range(48)]).reshape(128, 48, 1024) for l in range(L)])
    win_t = np.stack([_tile_w(w_in[l], in_cols).reshape(128, 36, 1024) for l in range(L)])
    wout_t = np.stack([_tile_w(w_out[l], [128 * j for j in range(8)]).reshape(128, 8, 1024) for l in range(L)])
    wup_t = np.stack([_tile_w(w_up[l], up_cols).reshape(128, 44, 1024) for l in range(L)])
    wdn_t = np.stack([_tile_w(w_down[l], [128 * j for j in range(8)]).reshape(128, 16, 1408) for l in range(L)])

    def col128(v):
        return v.reshape(-1, 128).T

    in_maps = []
    for i in range(NCORES):
        xs = [x_prompt[2 * i].T, x_prompt[2 * i + 1].T, x_sample[i].T]
        xTm = np.concatenate(xs, axis=1)
        xTm = np.ascontiguousarray(xTm.reshape(KC, 128, T).transpose(1, 0, 2))
        par = np.zeros((128, NPAR), np.float32)
        conds = np.stack([c_ctx, c[i]], axis=0)
        par[:, P_COND:P_COND + 16] = conds.reshape(2, KC, 128).transpose(2, 1, 0).reshape(128, 16)
        for l in range(L):
            par[:, P_ADAB + 48 * l:P_ADAB + 48 * (l + 1)] = col128(ada_b[l])
            par[:, P_NMIX + 8 * l:P_NMIX + 8 * (l + 1)] = col128(norm_mix[l])
            par[:, P_NFFN + 8 * l:P_NFFN + 8 * (l + 1)] = col128(norm_ffn[l])
            for d in range(2):
                par[:, P_LB + (l * 2 + d) * 4:P_LB + (l * 2 + d) * 4 + 4] = col128(hgrn_lb_logits[l, d])
                par[:, P_RD + (l * 2 + d) * 4:P_RD + (l * 2 + d) * 4 + 4] = ret_decay_logit[l, d][None, :]
            par[:, P_HN + 4 * l:P_HN + 4 * (l + 1)] = col128(hgrn_norm[l])
            par[:, P_RN + 4 * l:P_RN + 4 * (l + 1)] = col128(ret_norm[l])
        par[:, P_NFIN:P_NFIN + 8] = col128(norm_final)
        sh = state_hgrn[i].reshape(L * 2, H, 128, 128)
        sr = state_ret[i].reshape(L * 2, H, 128, 128)
        s0 = np.concatenate([sh, sr], axis=1).reshape(L * 2 * 8, 128, 128).transpose(1, 0, 2)
        in_maps.append({
            "xT": xTm, "params": par, "cst": cst, "rope": rope, "s0": np.ascontiguousarray(s0),
            "ada_w": adaw_t, "w_in": win_t, "w_out": wout_t, "w_up": wup_t, "w_down": wdn_t,
        })
    if _prepare_only:
        return in_maps
    if "nc" not in _CACHE:
        _CACHE["nc"] = build_program()
    nc = _CACHE["nc"]
    res = run_bass_kernel_spmd(nc, in_maps, core_ids=list(range(NCORES)))

    y_prompt = np.empty((16, SEQP, D), np.float32)
    y_sample = np.empty((8, SEQS, D), np.float32)
    nsh = np.empty((16, L, 2, H, 128, 128), np.float32)
    nsr = np.empty((16, L, 2, H, 128, 128), np.float32)
    for i in range(NCORES):
        r = res.results[i]
        yT = np.asarray(r["yT"])
        y = yT.transpose(2, 1, 0).reshape(T, D)
        y_prompt[2 * i] = y[0:256]
        y_prompt[2 * i + 1] = y[256:512]
        y_sample[i] = y[512:]
        st = np.asarray(r["st"]).reshape(128, 2, L, 2, 2, H, 128)
        st = st.transpose(1, 2, 3, 4, 5, 0, 6)
        for j in range(2):
            nsh[2 * i + j] = st[j, :, :, 0]
            nsr[2 * i + j] = st[j, :, :, 1]
    return (y_prompt, y_sample, nsh, nsr)
```

```python
import threading
import numpy as np
from contextlib import ExitStack
import concourse.bass as bass
import concourse.mybir as mybir
from concourse.bass_utils import run_bass_kernel_spmd

F32 = mybir.dt.float32
BF16 = mybir.dt.bfloat16
AF = mybir.ActivationFunctionType
ALU = mybir.AluOpType

NCORES = 8
L = 2
D = 1024
KC = 8
H = 4
DFF = 2816
NFF = 22
TB = 512
NBLK = 5
T = TB * NBLK
SEQP = 256
SEQS = 2048
EPS = 1e-6
LOG_FLOOR = float(np.log(1e-12))
LN_KSCALE = float(np.log(128.0 ** -0.5))

P_COND = 0
P_ADAB = 16
P_NMIX = 112
P_NFFN = 128
P_NFIN = 144
P_LB = 152
P_HN = 168
P_RN = 176
P_RD = 184
NPAR = 200
C_M32 = 0
C_M64 = 512
C_MASK = 1024
C_ID = 1152
C_RM = 1280
C_TP1 = 1408
C_TM = 1472
NCST = 1536

SLOT = 3072
NSLOT = 4
SAME_ENGINE_SYNC = True
LIST_SCHED = True


class Buf:
    __slots__ = ("ap", "w", "r", "excl")

    def __init__(self, ap, excl=False):
        self.ap = ap
        self.w = {}
        self.r = {}
        self.excl = excl


class Sched:
    def __init__(self, nc, es):
        self.nc = nc
        self.eng = {"pe": nc.tensor, "act": nc.scalar, "dve": nc.vector, "pool": nc.gpsimd, "sp": nc.sync}
        self.sems = {}
        self.val = {}
        for e in self.eng:
            self.sems[e] = es.enter_context(nc.semaphore("s_" + e))
            self.val[e] = 0
        self.dpool = {"sp": [], "pool": []}
        for q, n in (("sp", 12), ("pool", 8)):
            for i in range(n):
                k = "d_%s%d" % (q, i)
                self.sems[k] = es.enter_context(nc.semaphore(k))
                self.val[k] = 0
                self.dpool[q].append(k)
        self.dnext = {"sp": 0, "pool": 0}
        self.seen = {e: {} for e in self.eng}
        self.nops = 0
        self.hooks = {}
        self.efree = {e: 0.0 for e in self.eng}
        self.tfin = {}

    def _wait(self, e, key, v):
        if self.seen[e].get(key, 0) >= v:
            return
        self.eng[e].wait_ge(self.sems[key], v)
        self.seen[e][key] = v

    def _dep(self, e, k, v):
        if k == e and (e == "pe" or not SAME_ENGINE_SYNC):
            return
        self._wait(e, k, v)

    def _deps(self, e, reads, writes):
        for b in reads:
            for k, v in b.w.items():
                self._dep(e, k, v)
            if b.excl:
                for k, v in b.r.items():
                    if k != e:
                        self._dep(e, k, v)
        for b in writes:
            for k, v in b.w.items():
                self._dep(e, k, v)
            for k, v in b.r.items():
                self._dep(e, k, v)

    def _mark(self, key, v, reads, writes):
        for b in writes:
            if b.r:
                b.w = {}
                b.r = {}
            b.w[key] = v
        for b in reads:
            if b.r.get(key, 0) < v:
                b.r[key] = v

    def _cost(self, e, n, f):
        if e == "pe":
            return 0.03 + max(n, 64) * f / 1800.0
        if e == "act":
            return 0.22 + n * f / 1400.0
        if e == "dve":
            return 0.15 + n * f / 960.0
        if e == "pool":
            return 0.3 + n * f / 520.0
        return 0.05

    def _ready(self, e, reads, writes):
        t = 0.0
        for b in reads:
            for k, v in b.w.items():
                t = max(t, self.tfin.get((k, v), 0.0))
            if b.excl:
                for k, v in b.r.items():
                    t = max(t, self.tfin.get((k, v), 0.0))
        for b in writes:
            for k, v in b.w.items():
                t = max(t, self.tfin.get((k, v), 0.0))
            for k, v in b.r.items():
                t = max(t, self.tfin.get((k, v), 0.0))
        return t + 0.15

    def op(self, e, fn, reads=(), writes=(), n=512, f=1.0):
        h = self.hooks.get(threading.get_ident())
        if h is not None:
            h(lambda: max(self.efree[e], self._ready(e, reads, writes)))
        self._deps(e, reads, writes)
        ins = fn(self.eng[e])
        self.val[e] += 1
        ins.then_inc(self.sems[e], 1)
        self._mark(e, self.val[e], reads, writes)
        t0 = max(self.efree[e], self._ready(e, reads, writes))
        t1 = t0 + self._cost(e, n, f)
        self.efree[e] = t1
        self.tfin[(e, self.val[e])] = t1
        self.nops += 1

    def dma(self, q, out_ap, in_ap, reads=(), writes=(), mb=0.5):
        self._deps(q, reads, writes)
        keys = self.dpool[q]
        key = keys[self.dnext[q] % len(keys)]
        self.dnext[q] += 1
        if self.val[key] > 0:
            self._wait(q, key, self.val[key])
        ins = self.eng[q].dma_start(out=out_ap, in_=in_ap)
        self.val[key] += 16
        ins.then_inc(self.sems[key], 16)
        self._mark(key, self.val[key], reads, writes)
        t0 = max(self.efree[q], self._ready(q, reads, writes))
        self.efree[q] = t0 + 0.1
        self.tfin[(key, self.val[key])] = t0 + 2.0 + mb * 4.0
        return key, self.val[key]

    def barrier(self, engines=("pe", "act", "dve", "pool")):
        for e in engines:
            for k in engines:
                if k != e and self.val[k] > 0:
                    self._wait(e, k, self.val[k])

    def run_streams(self, fns, prio=None):
        n = len(fns)
        go = [threading.Semaphore(0) for _ in range(n)]
        back = threading.Semaphore(0)
        st = {"done": [False] * n, "err": None, "est": [None] * n}

        def mk(i):
            def hook(est=None):
                st["est"][i] = est
                back.release()
                go[i].acquire()

            def runner():
                go[i].acquire()
                self.hooks[threading.get_ident()] = hook
                try:
                    fns[i](hook)
                except BaseException as ex:
                    st["err"] = ex
                finally:
                    self.hooks.pop(threading.get_ident(), None)
                    st["done"][i] = True
                    back.release()
            t = threading.Thread(target=runner, daemon=True)
            t.start()
            return t

        ths = [mk(i) for i in range(n)]
        for i in range(n):
            go[i].release()
            back.acquire()
            if st["err"] is not None:
                raise st["err"]
        lastpick = [0] * n
        tick = 0
        while not all(st["done"]):
            best, bt = None, None
            for i in range(n):
                if st["done"][i]:
                    continue
                t = st["est"][i]() if st["est"][i] is not None else 1e30
                if prio is not None and 0.0 <= t < 1e29:
                    t = max(0.0, t - prio[i])
                if not LIST_SCHED and 0.0 <= t < 1e29:
                    t = 0.0
                if best is None or t < bt or (t == bt and lastpick[i] < lastpick[best]):
                    best, bt = i, t
            tick += 1
            lastpick[best] = tick
            go[best].release()
            back.acquire()
            if st["err"] is not None:
                raise st["err"]
        for t in ths:
            t.join()


class _Stop(Exception):
    pass


def build_program(stop=None, dbg_cols=0):
    nc = bass.Bass("TRN2", target_bir_lowering=False)
    dram = nc.dram_tensor
    xT_d = dram("xT", [128, KC, T], F32, kind="ExternalInput").ap()
    par_d = dram("params", [128, NPAR], F32, kind="ExternalInput").ap()
    cst_d = dram("cst", [128, NCST], F32, kind="ExternalInput").ap()
    rope_d = dram("rope", [128, 2, SEQS], F32, kind="ExternalInput").ap()
    s0_d = dram("s0", [128, L * 2 * 8, 128], F32, kind="ExternalInput").ap()
    adaw_d = dram("ada_w", [L, 128, 48, 1024], F32, kind="ExternalInput").ap()
    win_d = dram("w_in", [L, 128, 36, 1024], F32, kind="ExternalInput").ap()
    wout_d = dram("w_out", [L, 128, 8, 1024], F32, kind="ExternalInput").ap()
    wup_d = dram("w_up", [L, 128, 44, 1024], F32, kind="ExternalInput").ap()
    wdn_d = dram("w_down", [L, 128, 16, 1408], F32, kind="ExternalInput").ap()
    win_b = dram("w_in_b", [L, 128, 36, 1024], BF16, kind="Internal").ap()
    wout_b = dram("w_out_b", [L, 128, 8, 1024], BF16, kind="Internal").ap()
    wup_b = dram("w_up_b", [L, 128, 44, 1024], BF16, kind="Internal").ap()
    wdn_b = dram("w_down_b", [L, 128, 16, 1408], BF16, kind="Internal").ap()
    bnd_d = dram("bnd", [4, 128, 8, 128], F32, kind="Internal").ap()
    adaw_b = dram("ada_w_b", [L, 128, 48, 1024], BF16, kind="Internal").ap()
    yT_d = dram("yT", [128, KC, T], F32, kind="ExternalOutput").ap()
    st_d = dram("st", [128, 2 * L * 2 * 8, 128], F32, kind="ExternalOutput").ap()
    dbg_d = dram("dbg", [128, dbg_cols], F32, kind="ExternalOutput").ap() if dbg_cols else None

    es = ExitStack()
    with es:
        S = Sched(nc, es)
        out_tickets = []

        def ckpt(name, dump=None):
            if stop != name:
                return
            if dump is not None:
                ap, bufs = dump()
                n = ap.shape[-1]
                out_tickets.append(S.dma("pool", dbg_d[:, 0:n], ap, reads=bufs))
            raise _Stop()

        def sb(name, shape, dt):
            return es.enter_context(nc.sbuf_tensor(name, shape, dt))

        def ps(name, shape, dt):
            return es.enter_context(nc.psum_tensor(name, shape, dt))

        xT = sb("xT_s", [128, KC, T], F32)
        xTb = [Buf(xT[:, :, b * TB:(b + 1) * TB]) for b in range(NBLK)]
        par = Buf(sb("par_s", [128, NPAR], F32)[:, :])
        cst_t = sb("cst_s", [128, NCST], F32)
        cst = Buf(cst_t[:, :])
        ident_bf = Buf(sb("ident_bf", [128, 128], BF16)[:, :])
        rmat_bf = Buf(sb("rmat_bf", [128, 128], BF16)[:, :])
        ones1024 = Buf(sb("ones1024", [128, 128], F32)[:, :])
        ones128 = Buf(sb("ones128", [128, 128], F32)[:, :])
        ones1024b = Buf(sb("ones1024b", [128, 128], BF16)[:, :])
        ones128b = Buf(sb("ones128b", [128, 128], BF16)[:, :])
        small_t = sb("small_s", [128, 384], F32)
        small = Buf(small_t[:, :])
        scb = Buf(sb("scb_s", [128, 16], BF16)[:, :])
        rcst_t = sb("rcst_s", [128, 8, 3, 64], F32)
        rcst = Buf(rcst_t[:, :, :, :])
        ring_t = sb("ring_s", [128, NSLOT, SLOT], BF16)
        ring = [Buf(ring_t[:, i, :]) for i in range(NSLOT)]
        hT_t = sb("hT_s", [128, KC, TB], BF16)
        hT = Buf(hT_t[:, :, :])
        oc_t = sb("ocat_s", [128, KC, TB], BF16)
        ocat = Buf(oc_t[:, :, :])
        Sc_t = sb("Sc_s", [128, 8, 128], F32)
        Sc = [Buf(Sc_t[:, i, :]) for i in range(8)]
        Sin_t = sb("Sin_s", [128, 2, 128], F32)
        Sin = [Buf(Sin_t[:, i, :]) for i in range(2)]
        sin_n = [0]
        MS_N = 32832
        MS = sb("ms_s", [128, MS_N], BF16)

        def msf(off, n):
            return MS[:, off:off + 2 * n].bitcast(F32)

        Hs = []
        for i in range(2):
            o = i * 3584
            Hs.append({"qF": Buf(msf(o, 512)), "kF": Buf(msf(o + 1024, 512)), "vTf": Buf(MS[:, o + 2048:o + 2560]),
                       "vTr": Buf(MS[:, o + 2560:o + 3072]), "sgb": Buf(MS[:, o + 3072:o + 3584])})
        Us = []
        for i in range(2):
            o = 7168 + i * 7200
            lgb = Buf(msf(o, 512))
            Us.append({"lg": lgb, "e1": lgb, "kk": Buf(msf(o + 1024, 512)), "b32": Buf(msf(o + 2048, 512)),
                       "b64": Buf(msf(o + 3072, 512)), "e2": Buf(msf(o + 4096, 512)),
                       "Qt": Buf(MS[:, o + 5120:o + 5632]), "Qh": Buf(MS[:, o + 5632:o + 6144]),
                       "Kl": Buf(MS[:, o + 6144:o + 6656]), "Kh": Buf(MS[:, o + 6656:o + 7168]),
                       "dec": Buf(msf(o + 7168, 8))})
        o = 7168 + 14400
        VK = Buf(MS[:, o:o + 1024])
        Sall_ap = msf(o + 1024, 9 * 128)
        Sall = Buf(Sall_ap)
        Sbf = Buf(MS[:, o + 3328:o + 4352])
        Sfin = Buf(msf(o + 4352, 128))
        Am = Buf(MS[:, o + 4608:o + 5120])
        Amv = Am.ap.rearrange("p (j k) -> p j k", k=128)
        o += 5120
        fa = Buf(msf(o, 512))
        fb = Buf(msf(o + 1024, 512))
        fc = Buf(msf(o + 2048, 512))
        fd = Buf(msf(o + 3072, 512))
        o += 4096
        ropeb = Buf(msf(o, 1024))
        assert o + 2048 == MS_N
        actT_ap = MS[:, 0:NFF * TB]
        actT = Buf(actT_ap)
        yst_ap = msf(0, KC * TB)
        yst = Buf(yst_ap)

        pj = [Buf(ps("pj%d" % i, [128, 512], F32)[:, :], True) for i in range(2)]
        pA_t = ps("pA", [128, 4, 128], F32)
        pA_bank = Buf(pA_t[:, :, :], True)
        pS_t = ps("pS", [128, 8, 128], F32)
        pS = Buf(pS_t[:, :, :], True)
        pOf = Buf(ps("pOf", [128, 512], F32)[:, :], True)
        pOb = Buf(ps("pOb", [128, 512], F32)[:, :], True)
        pT_t = pA_t[:, :, :].rearrange("p a b -> p (a b)").bitcast(BF16).rearrange("p (j k) -> p j k", k=128)
        pT = pA_bank
        pF_t = ps("pF", [128, 512], F32)
        pF = Buf(pF_t[:, :], True)
        pjn = [0]

        def next_pj():
            pjn[0] += 1
            return pj[pjn[0] % 2]

        conv = {}

        def conv_chunks(l):
            ch = []
            for g in range(8):
                t0, n = (5 * g, 5) if g < 4 else (20 + 4 * (g - 4), 4)
                ch.append((("in", l, g), win_b[l, :, t0:t0 + n, :], win_d[l, :, t0:t0 + n, :]))
            ch.append((("out", l, 0), wout_b[l, :, :, :], wout_d[l, :, :, :]))
            for g in range(4):
                ch.append((("up", l, g), wup_b[l, :, 11 * g:11 * g + 11, :], wup_d[l, :, 11 * g:11 * g + 11, :]))
            for g in range(8):
                ch.append((("dn", l, g), wdn_b[l, :, 2 * g:2 * g + 2, :], wdn_d[l, :, 2 * g:2 * g + 2, :]))
            return ch

        def body():
            S.dma("sp", par.ap, par_d, writes=[par])
            S.dma("sp", cst.ap, cst_d, writes=[cst])
            conv_list = {}
            for l in range(L):
                conv_list[l] = conv_chunks(l)
                for key, ob, ib in conv_list[l]:
                    conv[key] = Buf(ob)
                for comp in range(6):
                    conv[("ada", l, comp)] = Buf(adaw_b[l, :, comp * 8:comp * 8 + 8, :])

            def issue_conv(l, i0, i1):
                for key, ob, ib in conv_list[l][i0:i1]:
                    S.dma("pool", ob, ib, writes=[conv[key]])

            issue_conv(0, 0, 8)

            S.op("dve", lambda e: e.tensor_copy(out=ident_bf.ap, in_=cst_t[:, C_ID:C_ID + 128]), reads=[cst], writes=[ident_bf])
            S.op("dve", lambda e: e.tensor_copy(out=rmat_bf.ap, in_=cst_t[:, C_RM:C_RM + 128]), reads=[cst], writes=[rmat_bf])
            S.op("dve", lambda e: e.memset(ones1024.ap, 1.0 / 1024.0), writes=[ones1024])
            S.op("dve", lambda e: e.memset(ones128.ap, 1.0 / 128.0), writes=[ones128])
            S.op("dve", lambda e: e.memset(ones1024b.ap, 1.0 / 1024.0), writes=[ones1024b])
            S.op("dve", lambda e: e.memset(ones128b.ap, 1.0 / 128.0), writes=[ones128b])
            m32_ap = cst_t[:, C_M32:C_M32 + 512]
            m64_ap = cst_t[:, C_M64:C_M64 + 512]
            mask_ap = cst_t[:, C_MASK:C_MASK + 128]
            tp1_ap = cst_t[:, C_TP1:C_TP1 + 64]
            tm_ap = cst_t[:, C_TM:C_TM + 64]

            O_SC = 0
            O_MOD = 16
            O_LB = 208
            O_OML = 224
            O_NOML = 240
            O_LGAM = 256
            O_NLGAM = 272
            O_RDEC = 288
            O_A = 304
            par_t = par.ap

            def sm(c0, n=1):
                return small_t[:, c0:c0 + n]

            S.op("act", lambda e: e.activation(out=sm(O_SC, 16), in_=par_t[:, P_COND:P_COND + 16], func=AF.Silu),
                 reads=[par], writes=[small])
            S.op("dve", lambda e: e.memset(sm(O_LB, 8), 0.0), writes=[small])
            S.op("dve", lambda e: e.tensor_tensor(out=sm(O_LB + 8, 8), in0=par_t[:, P_LB + 8:P_LB + 16],
                                                  in1=par_t[:, P_LB:P_LB + 8], op=ALU.subtract), reads=[par], writes=[small])
            S.op("act", lambda e: e.activation(out=sm(O_LB + 8, 8), in_=sm(O_LB + 8, 8), func=AF.Sigmoid),
                 reads=[small], writes=[small])
            S.op("dve", lambda e: e.tensor_scalar(out=sm(O_OML, 16), in0=sm(O_LB, 16), scalar1=-1.0, scalar2=1.0,
                                                  op0=ALU.mult, op1=ALU.add), reads=[small], writes=[small])
            S.op("dve", lambda e: e.tensor_scalar(out=sm(O_NOML, 16), in0=sm(O_LB, 16), scalar1=1.0, scalar2=-1.0,
                                                  op0=ALU.mult, op1=ALU.add), reads=[small], writes=[small])
            S.op("act", lambda e: e.activation(out=sm(O_LGAM, 16), in_=par_t[:, P_RD:P_RD + 16], func=AF.Sigmoid),
                 reads=[par], writes=[small])
            S.op("act", lambda e: e.activation(out=sm(O_LGAM, 16), in_=sm(O_LGAM, 16), func=AF.Ln),
                 reads=[small], writes=[small])
            S.op("dve", lambda e: e.tensor_scalar(out=sm(O_NLGAM, 16), in0=sm(O_LGAM, 16), scalar1=-1.0, scalar2=None,
                                                  op0=ALU.mult), reads=[small], writes=[small])
            S.op("act", lambda e: e.activation(out=sm(O_RDEC, 16), in_=sm(O_LGAM, 16), func=AF.Exp, scale=64.0),
                 reads=[small], writes=[small])
            S.op("dve", lambda e: e.tensor_copy(out=scb.ap, in_=sm(O_SC, 16)), reads=[small], writes=[scb])
            ckpt("pre", lambda: (small_t[:, :], [small]))

            def modcol(l, c, comp, kc):
                o_ = O_MOD + l * 96 + (comp * 8 + kc) * 2 + c
                return small_t[:, o_:o_ + 1]

            def acol(l, c, mf, kc):
                o_ = O_A + ((l * 2 + c) * 2 + mf) * 8 + kc
                return small_t[:, o_:o_ + 1]

            wsched = []
            wstate = {"issued": 0, "used": 0}

            def issue_loads(upto):
                while wstate["issued"] < min(upto, len(wsched)):
                    i = wstate["issued"]
                    kind, src, cb, n = wsched[i]
                    slot = ring[i % NSLOT]
                    if kind == "ada32":
                        dst = slot.ap[:, 0:2048].bitcast(F32)
                        S.dma("sp", dst, src, writes=[slot])
                    elif kind == "ada":
                        assert cb.w, "adaLN tile used before its conversion was issued"
                        S.dma("sp", slot.ap[:, 0:1024], src, reads=[cb], writes=[slot])
                    else:
                        if kind == "dn":
                            dst = slot.ap[:, 0:2816].rearrange("p (t k) -> p t k", k=1408)
                        else:
                            dst = slot.ap[:, 0:n * 1024].rearrange("p (t k) -> p t k", k=1024)
                        assert cb.w, "weight group used before its conversion was issued"
                        S.dma("sp", dst, src, reads=[cb], writes=[slot])
                    wstate["issued"] += 1

            def next_w(expect):
                i = wstate["used"]
                assert wsched[i][0] == expect[0] and wsched[i][3] == expect[1], (i, wsched[i][0], wsched[i][3], expect)
                issue_loads(i + NSLOT - 1)
                wstate["used"] += 1
                return ring[i % NSLOT]

            def wtile(slot, t):
                return slot.ap[:, t * 1024:(t + 1) * 1024]

            ada_todo = [(0, j) for j in range(16, 48)] + [(1, j) for j in range(48)]
            ada_done = {0: 0, 1: 0}

            def ada_tile(l, j, fp32=False):
                slot = next_w(("ada32" if fp32 else "ada", 1))
                pm = next_pj()
                if fp32:
                    wf = slot.ap[:, 0:2048].bitcast(F32)
                    for kc in range(KC):
                        S.op("pe", lambda e: e.matmul(pm.ap[:, 0:2], lhsT=wf[:, kc * 128:(kc + 1) * 128],
                                                      rhs=sm(O_SC + 2 * kc, 2), start=(kc == 0), stop=(kc == KC - 1)),
                             reads=[slot, small], writes=[pm], n=64, f=4.0)
                else:
                    for kc in range(KC):
                        S.op("pe", lambda e: e.matmul(pm.ap[:, 0:2], lhsT=slot.ap[:, kc * 128:(kc + 1) * 128],
                                                      rhs=scb.ap[:, 2 * kc:2 * kc + 2], start=(kc == 0), stop=(kc == KC - 1)),
                             reads=[slot, scb], writes=[pm], n=64)
                o_ = O_MOD + l * 96 + 2 * j
                S.op("dve", lambda e: e.tensor_scalar(out=sm(o_, 2), in0=pm.ap[:, 0:2],
                                                      scalar1=par_t[:, P_ADAB + l * 48 + j:P_ADAB + l * 48 + j + 1],
                                                      scalar2=None, op0=ALU.add), reads=[pm, par], writes=[small])
                ada_done[l] += 1
                for mf, comp in ((0, 1), (1, 4)):
                    if j == comp * 8 + 7:
                        for c in range(2):
                            scv = small_t[:, O_MOD + l * 96 + comp * 16 + c:O_MOD + l * 96 + comp * 16 + 16:2]
                            nw0 = (P_NMIX if mf == 0 else P_NFFN) + l * 8
                            S.op("dve", lambda e: e.scalar_tensor_tensor(
                                out=sm(O_A + ((l * 2 + c) * 2 + mf) * 8, 8), in0=scv, scalar=1.0, in1=par_t[:, nw0:nw0 + 8],
                                op0=ALU.add, op1=ALU.mult), reads=[small, par], writes=[small])

            def ada_some(n):
                for _ in range(n):
                    if ada_todo:
                        l_, j_ = ada_todo.pop(0)
                        ada_tile(l_, j_)

            def xcols(b, kc):
                return xT[:, kc, b * TB:(b + 1) * TB]

            def rstd_from(pn, dst):
                S.op("act", lambda e: e.activation(out=dst.ap, in_=pn.ap, func=AF.Ln, bias=EPS, scale=1.0),
                     reads=[pn], writes=[dst])
                S.op("act", lambda e: e.activation(out=dst.ap, in_=dst.ap, func=AF.Exp, scale=-0.5),
                     reads=[dst], writes=[dst])

            def ssq_block(b):
                pn = next_pj()
                sq = [fa, fb]
                for kc in range(KC):
                    s = sq[kc % 2]
                    sbf = s.ap.bitcast(BF16)[:, 0:TB]
                    S.op("act", lambda e: e.activation(out=sbf, in_=xcols(b, kc), func=AF.Square),
                         reads=[xTb[b]], writes=[s])
                    S.op("pe", lambda e: e.matmul(pn.ap, lhsT=ones1024b.ap, rhs=sbf, start=(kc == 0), stop=(kc == KC - 1)),
                         reads=[s, ones1024b], writes=[pn])
                rstd_from(pn, fc)

            def norm_block(b, l, c, mf):
                ssq_block(b)
                tmp = [fd, fa]
                shcomp = 0 if mf == 0 else 3
                for kc in range(KC):
                    t = tmp[kc % 2]
                    S.op("dve", lambda e: e.scalar_tensor_tensor(out=t.ap, in0=xcols(b, kc), scalar=acol(l, c, mf, kc),
                                                                 in1=fc.ap, op0=ALU.mult, op1=ALU.mult),
                         reads=[xTb[b], fc, small], writes=[t])
                    S.op("act", lambda e: e.activation(out=hT_t[:, kc, :], in_=t.ap, func=AF.Identity,
                                                       bias=modcol(l, c, shcomp, kc), scale=1.0),
                         reads=[t, small], writes=[hT])

            def proj(slot, t, pb):
                w = wtile(slot, t)
                for kc in range(KC):
                    S.op("pe", lambda e: e.matmul(pb.ap, lhsT=w[:, kc * 128:(kc + 1) * 128], rhs=hT_t[:, kc, :],
                                                  start=(kc == 0), stop=(kc == KC - 1)),
                         reads=[slot, hT], writes=[pb])

            def rev(ap):
                return ap[:, ::-1]

            def gate_evac(pb, l, d, h, reverse, U):
                o_ = (l * 2 + d) * 4 + h
                src = rev(pb.ap) if reverse else pb.ap
                e2, lg, kk = U["e2"], U["lg"], U["kk"]
                S.op("act", lambda e: e.activation(out=e2.ap, in_=src, func=AF.Exp, scale=-1.0), reads=[pb], writes=[e2])
                S.op("act", lambda e: e.activation(out=e2.ap, in_=e2.ap, func=AF.Ln, bias=1.0, scale=1.0), reads=[e2], writes=[e2])
                S.op("act", lambda e: e.activation(out=e2.ap, in_=e2.ap, func=AF.Exp, scale=-1.0), reads=[e2], writes=[e2])
                S.op("act", lambda e: e.activation(out=lg.ap, in_=e2.ap, func=AF.Ln, scale=sm(O_OML + o_), bias=sm(O_LB + o_)),
                     reads=[e2, small], writes=[lg])
                S.op("pool", lambda e: e.tensor_scalar(out=kk.ap, in0=e2.ap, scalar1=sm(O_NOML + o_), scalar2=sm(O_OML + o_),
                                                       op0=ALU.mult, op1=ALU.add), reads=[e2, small], writes=[kk])
                S.op("pool", lambda e: e.tensor_scalar(out=lg.ap, in0=lg.ap, scalar1=0.0, scalar2=LOG_FLOOR, op0=ALU.min, op1=ALU.max),
                     reads=[lg], writes=[lg])

            kh_ready = [0]

            def c3(ap, n=64):
                return ap.rearrange("p (c t) -> p c t", t=n)

            def hgrn_prep(qF, reverse, states_only, U, part=0):
                lg, kk, b32, b64, e1, e2 = U["lg"], U["kk"], U["b32"], U["b64"], U["e1"], U["e2"]
                Qt, Qh, Kl, Kh, decb = U["Qt"], U["Qh"], U["Kl"], U["Kh"], U["dec"]
                if part in (0, 1):
                    hgrn_prep1(U)
                if states_only or part == 1:
                    return
                hgrn_prep2(qF, reverse, U)

            def hgrn_prep1(U):
                lg, kk, b32, b64, e1, e2 = U["lg"], U["kk"], U["b32"], U["b64"], U["e1"], U["e2"]
                Qt, Qh, Kl, Kh, decb = U["Qt"], U["Qh"], U["Kl"], U["Kh"], U["dec"]
                S.op("dve", lambda e: e.tensor_tensor_scan(out=b64.ap, data0=m64_ap, data1=lg.ap, initial=0.0,
                                                           op0=ALU.mult, op1=ALU.add), reads=[cst, lg], writes=[b64], f=2.2)
                tot64 = c3(b64.ap)[:, :, 63]
                S.op("act", lambda e: e.activation(out=decb.ap, in_=tot64, func=AF.Exp), reads=[b64], writes=[decb], n=8)
                S.op("dve", lambda e: e.tensor_tensor(out=c3(e2.ap), in0=tot64.unsqueeze(2).to_broadcast([128, 8, 64]),
                                                      in1=c3(b64.ap), op=ALU.subtract), reads=[b64], writes=[e2])
                S.op("act", lambda e: e.activation(out=e2.ap, in_=e2.ap, func=AF.Exp), reads=[e2], writes=[e2])
                S.op("dve", lambda e: e.tensor_tensor(out=Kh.ap, in0=kk.ap, in1=e2.ap, op=ALU.mult),
                     reads=[kk, e2], writes=[Kh])
                kh_ready[0] += 1

            def hgrn_prep2(qF, reverse, U):
                lg, kk, b32, b64, e1, e2 = U["lg"], U["kk"], U["b32"], U["b64"], U["e1"], U["e2"]
                Qt, Qh, Kl, Kh, decb = U["Qt"], U["Qh"], U["Kl"], U["Kh"], U["dec"]
                qsrc_ap = rev(qF.ap) if reverse else qF.ap
                S.op("dve", lambda e: e.tensor_tensor_scan(out=b32.ap, data0=m32_ap, data1=lg.ap, initial=0.0,
                                                           op0=ALU.mult, op1=ALU.add), reads=[cst, lg], writes=[b32], f=2.2)
                S.op("act", lambda e: e.activation(out=e1.ap, in_=b32.ap, func=AF.Exp), reads=[b32], writes=[e1])
                S.op("pool", lambda e: e.tensor_tensor(out=Qt.ap, in0=qsrc_ap, in1=e1.ap, op=ALU.mult),
                     reads=[qF, e1], writes=[Qt])
                S.op("act", lambda e: e.activation(out=e2.ap, in_=b64.ap, func=AF.Exp), reads=[b64], writes=[e2])
                S.op("pool", lambda e: e.tensor_tensor(out=Qh.ap, in0=qsrc_ap, in1=e2.ap, op=ALU.mult),
                     reads=[qF, e2], writes=[Qh])
                S.op("act", lambda e: e.activation(out=e1.ap, in_=b32.ap, func=AF.Exp, scale=-1.0), reads=[b32], writes=[e1])
                S.op("pool", lambda e: e.tensor_tensor(out=Kl.ap, in0=kk.ap, in1=e1.ap, op=ALU.mult),
                     reads=[kk, e1], writes=[Kl])

            def ret_prep(Hd, reverse, l, d, h, states_only, U, part=0):
                i = d * 4 + h
                qF, kF = Hd["qF"], Hd["kF"]
                ksrc_ap = rev(kF.ap) if reverse else kF.ap
                if part in (0, 1):
                    S.op("pool", lambda e: e.tensor_tensor(out=c3(U["Kh"].ap), in0=c3(ksrc_ap),
                                                          in1=rcst_t[:, i, 2, :].unsqueeze(1).to_broadcast([128, 8, 64]),
                                                          op=ALU.mult), reads=[kF, rcst], writes=[U["Kh"]])
                    kh_ready[0] += 1
                if states_only or part == 1:
                    return
                qsrc_ap = rev(qF.ap) if reverse else qF.ap
                S.op("pool", lambda e: e.tensor_tensor(out=c3(U["Qt"].ap), in0=c3(qsrc_ap),
                                                      in1=rcst_t[:, i, 0, :].unsqueeze(1).to_broadcast([128, 8, 64]),
                                                      op=ALU.mult), reads=[qF, rcst], writes=[U["Qt"]])
                S.op("pool", lambda e: e.tensor_tensor(out=c3(U["Kl"].ap), in0=c3(ksrc_ap),
                                                      in1=rcst_t[:, i, 1, :].unsqueeze(1).to_broadcast([128, 8, 64]),
                                                      op=ALU.mult), reads=[kF, rcst], writes=[U["Kl"]])

            VKv = VK.ap.rearrange("p (j k) -> p j k", k=128)
            Sallv = Sall_ap.rearrange("p (c k) -> p c k", k=128)
            Sbfv = Sbf.ap.rearrange("p (c k) -> p c k", k=128)

            def scan_core(vT, U, Qh_b, dec_fn, init, two_level, pO, states_only, prompt, fin_fn, gate_full=None):
                Qt, Kl, Kh, decb = U["Qt"], U["Kl"], U["Kh"], U["dec"]
                for j in range(4):
                    S.op("pe", lambda e: e.transpose(pT_t[:, j, :], vT.ap[:, j * 128:(j + 1) * 128], ident_bf.ap),
                         reads=[vT, ident_bf], writes=[pT], n=128)
                for j in range(4):
                    S.op("pe", lambda e: e.transpose(pT_t[:, 4 + j, :], Kh.ap[:, j * 128:(j + 1) * 128], ident_bf.ap),
                         reads=[Kh, ident_bf], writes=[pT], n=128)
                S.op("act", lambda e: e.activation(out=VKv, in_=pT_t[:, :, :], func=AF.Copy), reads=[pT], writes=[VK], n=1024)
                for c in range(8):
                    j, po = c // 2, 64 * (c % 2)
                    S.op("pe", lambda e: e.matmul(pS_t[:, (c % 2) * 4 + j, :], lhsT=VKv[po:po + 64, 4 + j, :],
                                                  rhs=VKv[po:po + 64, j, :], start=True, stop=True), reads=[VK], writes=[pS], n=128)
                if init[0] == "buf":
                    S.op("dve", lambda e: e.tensor_copy(out=Sallv[:, 0, :], in_=init[1].ap), reads=[init[1]], writes=[Sall], n=128)
                elif init[0] == "zero":
                    S.op("dve", lambda e: e.memset(Sallv[:, 0, :], 0.0), writes=[Sall], n=128)
                else:
                    sin_n[0] += 1
                    sb_ = Sin[sin_n[0] % 2]
                    S.dma("sp", sb_.ap, init[1], reads=([init[2]] if init[2] is not None else []), writes=[sb_])
                    S.op("dve", lambda e: e.tensor_copy(out=Sallv[:, 0, :], in_=sb_.ap), reads=[sb_], writes=[Sall], n=128)
                if prompt:
                    S.op("dve", lambda e: e.memset(Sallv[:, 4, :], 0.0), writes=[Sall], n=128)
                for c in range(8):
                    if prompt and c == 3:
                        dst_ap, dst_b = Sfin.ap, Sfin
                    else:
                        dst_ap, dst_b = Sallv[:, c + 1, :], Sall
                    S.op("dve", lambda e: e.scalar_tensor_tensor(out=dst_ap, in0=Sallv[:, c, :], scalar=dec_fn(c),
                                                                 in1=pS_t[:, (c % 2) * 4 + c // 2, :], op0=ALU.mult, op1=ALU.add),
                         reads=[Sall, pS, decb, small], writes=[dst_b], n=128)
                if prompt:
                    fin_fn(0, Sfin, Sfin.ap)
                fin_fn(1, Sall, Sallv[:, 8, :])
                if states_only:
                    return
                S.op("act", lambda e: e.activation(out=Sbf.ap, in_=Sall_ap[:, 0:1024], func=AF.Copy), reads=[Sall], writes=[Sbf], n=1024)
                if gate_full is not None:
                    gate_full()
                if not states_only:
                    for j in range(4):
                        cs = slice(j * 128, (j + 1) * 128)
                        S.op("pe", lambda e: e.matmul(pA_t[:, j, :], lhsT=Kl.ap[:, cs], rhs=Qt.ap[:, cs], start=True, stop=True),
                             reads=[Kl, Qt], writes=[pA_bank], n=128)
                        if two_level:
                            for hh in range(2):
                                o_ = j * 128 + hh * 64
                                S.op("pe", lambda e: e.matmul(pA_t[hh * 64:hh * 64 + 32, j, hh * 64 + 32:hh * 64 + 64],
                                                              lhsT=Kl.ap[:, o_:o_ + 32], rhs=Qh_b.ap[:, o_ + 32:o_ + 64],
                                                              start=True, stop=True), reads=[Kl, Qh_b], writes=[pA_bank], n=32)
                    S.op("dve", lambda e: e.tensor_tensor(out=Amv, in0=pA_t[:, :, :],
                                                          in1=mask_ap.unsqueeze(1).to_broadcast([128, 4, 128]), op=ALU.mult),
                         reads=[pA_bank, cst], writes=[Am])

                    for j in range(4):
                        cs = slice(j * 128, (j + 1) * 128)
                        S.op("pe", lambda e: e.matmul(pO.ap[:, cs], lhsT=VKv[:, j, :], rhs=Amv[:, j, :], start=(j == 0), stop=False),
                             reads=[VK, Am], writes=[pO], n=128)
                for c in range(8):
                    S.op("pe", lambda e: e.matmul(pO.ap[:, c * 64:(c + 1) * 64], lhsT=Sbfv[:, c, :],
                                                  rhs=Qh_b.ap[:, c * 64:(c + 1) * 64], start=False, stop=(c == 7)),
                         reads=[Sbf, Qh_b], writes=[pO], n=64)

            def finalize_evac():
                S.op("act", lambda e: e.activation(out=fa.ap, in_=rev(pOb.ap), func=AF.Copy), reads=[pOb], writes=[fa])
                S.op("dve", lambda e: e.tensor_tensor(out=fb.ap, in0=pOf.ap, in1=fa.ap, op=ALU.add),
                     reads=[pOf, fa], writes=[fb])

            def finalize_rest(l, mixer, h, Hd):
                center = (mixer == 1)
                sgb = Hd["sgb"]
                if center:
                    S.op("pe", lambda e: e.matmul(pF.ap, lhsT=ones128.ap, rhs=fb.ap, start=True, stop=True),
                         reads=[ones128, fb], writes=[pF], f=4.0)
                    S.op("dve", lambda e: e.tensor_tensor(out=fb.ap, in0=fb.ap, in1=pF.ap, op=ALU.subtract),
                         reads=[fb, pF], writes=[fb])
                fabf = fa.ap.bitcast(BF16)[:, 0:TB]
                S.op("act", lambda e: e.activation(out=fabf, in_=fb.ap, func=AF.Square), reads=[fb], writes=[fa])
                S.op("pe", lambda e: e.matmul(pF.ap, lhsT=ones128b.ap, rhs=fabf, start=True, stop=True),
                     reads=[ones128b, fa], writes=[pF])
                S.op("act", lambda e: e.activation(out=fc.ap, in_=pF.ap, func=AF.Ln, bias=EPS, scale=1.0),
                     reads=[pF], writes=[fc])
                S.op("act", lambda e: e.activation(out=fc.ap, in_=fc.ap, func=AF.Exp, scale=-0.5),
                     reads=[fc], writes=[fc])
                S.op("pool", lambda e: e.tensor_tensor(out=fd.ap, in0=fb.ap, in1=fc.ap, op=ALU.mult),
                     reads=[fb, fc], writes=[fd])
                g0 = (P_HN if mixer == 0 else P_RN) + l * 4 + h
                S.op("dve", lambda e: e.scalar_tensor_tensor(out=oc_t[:, mixer * 4 + h, :], in0=fd.ap, scalar=par_t[:, g0:g0 + 1],
                                                             in1=sgb.ap, op0=ALU.mult, op1=ALU.mult),
                     reads=[fd, par, sgb], writes=[ocat])

            def rope_evac(pb, dst, do_rope, U):
                if not do_rope:
                    S.op("act", lambda e: e.activation(out=dst.ap, in_=pb.ap, func=AF.Copy), reads=[pb], writes=[dst])
                    return
                Qt, e1, e2 = U["Qt"], U["e1"], U["e2"]
                S.op("act", lambda e: e.activation(out=Qt.ap, in_=pb.ap, func=AF.Copy), reads=[pb], writes=[Qt])
                pr = next_pj()
                S.op("pe", lambda e: e.matmul(pr.ap, lhsT=rmat_bf.ap, rhs=Qt.ap, start=True, stop=True),
                     reads=[rmat_bf, Qt], writes=[pr])
                rp = ropeb.ap.rearrange("p (a t) -> p a t", a=2)
                S.op("dve", lambda e: e.tensor_tensor(out=e1.ap, in0=pb.ap, in1=rp[:, 0, :], op=ALU.mult),
                     reads=[pb, ropeb], writes=[e1])
                S.op("dve", lambda e: e.tensor_tensor(out=e2.ap, in0=pr.ap, in1=rp[:, 1, :], op=ALU.mult),
                     reads=[pr, ropeb], writes=[e2])
                S.op("pool", lambda e: e.tensor_tensor(out=dst.ap, in0=e1.ap, in1=e2.ap, op=ALU.add),
                     reads=[e1, e2], writes=[dst])

            def v_evac(pb, want_fwd, Hd):
                if want_fwd:
                    S.op("act", lambda e: e.activation(out=Hd["vTf"].ap, in_=pb.ap, func=AF.Copy), reads=[pb], writes=[Hd["vTf"]])
                S.op("act", lambda e: e.activation(out=Hd["vTr"].ap, in_=rev(pb.ap), func=AF.Copy), reads=[pb], writes=[Hd["vTr"]])

            def load_rope(b):
                p0 = (b - 1) * TB
                S.dma("sp", ropeb.ap.rearrange("p (a t) -> p a t", a=2), rope_d[:, :, p0:p0 + TB], writes=[ropeb])

            def ret_consts(l):
                for d in range(2):
                    for h in range(4):
                        i = d * 4 + h
                        o_ = (l * 2 + d) * 4 + h
                        S.op("act", lambda e: e.activation(out=rcst_t[:, i, 0, :], in_=tp1_ap, func=AF.Exp, scale=sm(O_LGAM + o_)),
                             reads=[cst, small], writes=[rcst])
                        S.op("act", lambda e: e.activation(out=rcst_t[:, i, 1, :], in_=tp1_ap, func=AF.Exp, scale=sm(O_NLGAM + o_),
                                                           bias=LN_KSCALE), reads=[cst, small], writes=[rcst])
                        S.op("act", lambda e: e.activation(out=rcst_t[:, i, 2, :], in_=tm_ap, func=AF.Exp, scale=sm(O_LGAM + o_),
                                                           bias=LN_KSCALE), reads=[cst, small], writes=[rcst])

            def st_index(j, l, d, mh):
                return ((j * L + l) * 2 + d) * 8 + mh

            bndb = [[Buf(bnd_d[b, :, mh, :]) for mh in range(8)] for b in range(4)]

            def run_units(units, heads=None):
                prog = {"a": 0, "b": 0, "c": 0}
                n = len(units)

                def gate(hook, cond):
                    while cond():
                        hook(lambda: 1e30 if cond() else -1.0)

                def stream_a(hook):
                    for k in range(n):
                        gate(hook, lambda: prog["b"] < k - 1)
                        if heads is not None and k % 2 == 0:
                            gate(hook, lambda: prog["c"] < k // 2 - 1)
                        units[k][0](k)
                        prog["a"] = k + 1

                kh0 = kh_ready[0]

                def stream_b(hook):
                    for k in range(n):
                        gate(hook, lambda: kh_ready[0] - kh0 < k + 1 and prog["a"] < k + 1)
                        if heads is not None and k % 2 == 1:
                            gate(hook, lambda: prog["c"] < k // 2)
                        units[k][1](k, lambda: gate(hook, lambda: prog["a"] < k + 1))
                        prog["b"] = k + 1

                def stream_c(hook):
                    for h in range(len(heads)):
                        gate(hook, lambda: prog["b"] < 2 * h + 2)
                        heads[h](h)
                        prog["c"] = h + 1

                S.run_streams([stream_a, stream_b] + ([stream_c] if heads is not None else []),
                              prio=[0.5, 0.0] + ([0.0] if heads is not None else []))

            def pass1(l):
                for mh in range(8):
                    S.dma("sp", Sc[mh].ap, s0_d[:, (l * 2 + 1) * 8 + mh, :], writes=[Sc[mh]])
                for b in (4, 3, 2):
                    norm_block(b, l, 1, 0)
                    load_rope(b)
                    units = []
                    for mh in range(8):
                        mixer, h = mh // 4, mh % 4

                        def a_fn(k, mixer=mixer, h=h):
                            U, Hd = Us[k % 2], Hs[k % 2]
                            pop_conv()
                            slot = next_w(("in", 2))
                            pz = next_pj()
                            proj(slot, 0, pz)
                            if mixer == 0:
                                gate_evac(pz, l, 1, h, True, U)
                            else:
                                rope_evac(pz, Hd["kF"], True, U)
                            pv = next_pj()
                            proj(slot, 1, pv)
                            v_evac(pv, False, Hd)
                            if mixer == 0:
                                hgrn_prep(None, True, True, U)
                            else:
                                ret_prep(Hd, True, l, 1, h, True, U)
                            ada_some(2)

                        def b_fn(k, gate_full, mixer=mixer, h=h, mh=mh, b=b):
                            U, Hd = Us[k % 2], Hs[k % 2]
                            if mixer == 0:
                                dec_fn = (lambda c: U["dec"].ap[:, c:c + 1])
                            else:
                                dec_fn = (lambda c, o_=(l * 2 + 1) * 4 + h: sm(O_RDEC + o_))

                            def fin(which, sbuf, sap):
                                S.op("dve", lambda e: e.tensor_copy(out=Sc[mh].ap, in_=sap), reads=[sbuf], writes=[Sc[mh]], n=128)
                                S.dma("sp", bnd_d[b - 1, :, mh, :], sap, reads=[sbuf], writes=[bndb[b - 1][mh]])
                            scan_core(Hd["vTr"], U, None, dec_fn, ("buf", Sc[mh]), False, None, True, False, fin)
                        units.append((a_fn, b_fn))
                    run_units(units)
                    if l == 0 and not deferred["l0_rest"]:
                        deferred["l0_rest"] = True
                        for comp in range(6):
                            deferred_conv.append((adaw_b[1, :, comp * 8:comp * 8 + 8, :], adaw_d[1, :, comp * 8:comp * 8 + 8, :],
                                                  conv[("ada", 1, comp)]))
                        for key, ob, ib in conv_list[0][8:]:
                            deferred_conv.append((ob, ib, conv[key]))
                    if l == 0 and b == 2:
                        while deferred_conv:
                            pop_conv()

            def pass2(l, last):
                for mh in range(8):
                    S.dma("sp", Sc[mh].ap, s0_d[:, (l * 2 + 0) * 8 + mh, :], writes=[Sc[mh]])
                for b in range(NBLK):
                    prompt = (b == 0)
                    c = 0 if prompt else 1
                    norm_block(b, l, c, 0)
                    if not prompt:
                        load_rope(b)
                    units = []
                    hold = {}
                    for u in range(16):
                        mh, d = u // 2, u % 2
                        mixer, h = mh // 4, mh % 4

                        def a_fn(k, mixer=mixer, h=h, mh=mh, d=d):
                            U, Hd = Us[k % 2], Hs[mh % 2]
                            if d == 0:
                                ada_some(4)
                                s1 = next_w(("in", 3))
                                s2 = next_w(("in", 2 if mixer == 0 else 1))
                                hold["s1"] = s1
                                if mixer == 0:
                                    pz = next_pj()
                                    proj(s1, 1, pz)
                                    gate_evac(pz, l, 0, h, False, U)
                                    pv = next_pj()
                                    proj(s2, 0, pv)
                                    v_evac(pv, True, Hd)
                                    hgrn_prep(None, False, False, U, part=1)
                                    pq = next_pj()
                                    proj(s1, 0, pq)
                                    S.op("act", lambda e: e.activation(out=Hd["qF"].ap, in_=pq.ap, func=AF.Silu),
                                         reads=[pq], writes=[Hd["qF"]])
                                    pg = next_pj()
                                    proj(s2, 1, pg)
                                    S.op("act", lambda e: e.activation(out=Hd["sgb"].ap, in_=pg.ap, func=AF.Silu),
                                         reads=[pg], writes=[Hd["sgb"]])
                                    hgrn_prep2(Hd["qF"], False, U)
                                    return
                                pv = next_pj()
                                proj(s1, 2, pv)
                                v_evac(pv, True, Hd)
                                pk = next_pj()
                                proj(s1, 1, pk)
                                rope_evac(pk, Hd["kF"], not prompt, U)
                                ret_prep(Hd, False, l, 0, h, False, U, part=1)
                                pq = next_pj()
                                proj(s1, 0, pq)
                                rope_evac(pq, Hd["qF"], not prompt, U)
                                pg = next_pj()
                                proj(s2, 0, pg)
                                S.op("act", lambda e: e.activation(out=Hd["sgb"].ap, in_=pg.ap, func=AF.Silu),
                                     reads=[pg], writes=[Hd["sgb"]])
                                ret_prep(Hd, False, l, 0, h, False, U, part=2)
                                return
                            if mixer == 0:
                                pz = next_pj()
                                proj(hold["s1"], 1 + d, pz)
                                gate_evac(pz, l, d, h, d == 1, U)
                                hgrn_prep(Hd["qF"], d == 1, False, U)
                            else:
                                ret_prep(Hd, d == 1, l, d, h, False, U)

                        def b_fn(k, gate_full, mixer=mixer, h=h, mh=mh, d=d, b=b, prompt=prompt):
                            U, Hd = Us[k % 2], Hs[mh % 2]
                            if mixer == 0:
                                dec_fn = (lambda cc: U["dec"].ap[:, cc:cc + 1])
                                Qh_b = U["Qh"]
                            else:
                                dec_fn = (lambda cc, o_=(l * 2 + d) * 4 + h: sm(O_RDEC + o_))
                                Qh_b = U["Qt"]
                            if prompt:
                                init = ("zero",)
                            elif d == 0:
                                init = ("buf", Sc[mh])
                            elif b == NBLK - 1:
                                init = ("dram", s0_d[:, (l * 2 + 1) * 8 + mh, :], None)
                            else:
                                init = ("dram", bnd_d[b, :, mh, :], bndb[b][mh])

                            def fin(which, sbuf, sap):
                                if prompt:
                                    jj = which if d == 0 else 1 - which
                                    out_tickets.append(S.dma("sp", st_d[:, st_index(jj, l, d, mh), :], sap, reads=[sbuf]))
                                elif d == 0 and b < NBLK - 1:
                                    S.op("dve", lambda e: e.tensor_copy(out=Sc[mh].ap, in_=sap), reads=[sbuf], writes=[Sc[mh]], n=128)
                            scan_core(Hd["vTf"] if d == 0 else Hd["vTr"], U, Qh_b, dec_fn, init, mixer == 0,
                                      pOf if d == 0 else pOb, False, prompt, fin, gate_full)
                            if d == 1:
                                finalize_evac()
                        units.append((a_fn, b_fn))
                    heads = []
                    for mh in range(8):
                        def c_fn(hh_, mh=mh):
                            finalize_rest(l, mh // 4, mh % 4, Hs[mh % 2])
                            ckpt("p2h%d_b%d" % (mh, b), lambda: (oc_t[:, mh, :], [ocat]))
                        heads.append(c_fn)
                    run_units(units, heads)
                    for f in range(KC):
                        if f % 3 == 0:
                            slot = next_w(("out", 3 if f < 6 else 2))
                        t = f % 3
                        po = next_pj()
                        w = wtile(slot, t)
                        for kc in range(KC):
                            S.op("pe", lambda e: e.matmul(po.ap, lhsT=w[:, kc * 128:(kc + 1) * 128], rhs=oc_t[:, kc, :],
                                                          start=(kc == 0), stop=(kc == KC - 1)), reads=[slot, ocat], writes=[po])
                        S.op("dve", lambda e: e.scalar_tensor_tensor(out=xcols(b, f), in0=po.ap, scalar=modcol(l, c, 2, f),
                                                                     in1=xcols(b, f), op0=ALU.mult, op1=ALU.add),
                             reads=[po, small, xTb[b]], writes=[xTb[b]])
                    ckpt("p2mix_b%d" % b, lambda: (xT[:, 0, b * TB:(b + 1) * TB], [xTb[b]]))
                    norm_block(b, l, c, 1)
                    S.barrier()
                    actv = actT_ap.rearrange("p (j t) -> p j t", t=TB)
                    for j in range(NFF):
                        slot = next_w(("up", 2))
                        pgate = next_pj()
                        proj(slot, 0, pgate)
                        sgt = (fa, fb)[j % 2]
                        S.op("act", lambda e: e.activation(out=sgt.ap, in_=pgate.ap, func=AF.Silu), reads=[pgate], writes=[sgt])
                        pup = next_pj()
                        proj(slot, 1, pup)
                        S.op("dve", lambda e: e.tensor_tensor(out=actv[:, j, :], in0=pup.ap, in1=sgt.ap, op=ALU.mult),
                             reads=[pup, sgt], writes=[actT])
                    for f in range(KC):
                        slot = next_w(("dn", -1))
                        po = next_pj()
                        for kc in range(NFF):
                            S.op("pe", lambda e: e.matmul(po.ap, lhsT=slot.ap[:, kc * 128:(kc + 1) * 128], rhs=actv[:, kc, :],
                                                          start=(kc == 0), stop=(kc == NFF - 1)), reads=[slot, actT], writes=[po])
                        S.op("dve", lambda e: e.scalar_tensor_tensor(out=xcols(b, f), in0=po.ap, scalar=modcol(l, c, 5, f),
                                                                     in1=xcols(b, f), op0=ALU.mult, op1=ALU.add),
                             reads=[po, small, xTb[b]], writes=[xTb[b]])
                    S.barrier()
                    ckpt("p2ffn_b%d" % b, lambda: (xT[:, 0, b * TB:(b + 1) * TB], [xTb[b]]))
                    if l + 1 < L:
                        n1 = len(conv_list[l + 1])
                        issue_conv(l + 1, (b * n1) // 3 if b < 3 else n1, ((b + 1) * n1) // 3 if b < 3 else n1)
                    if last:
                        ssq_block(b)
                        ystv = yst_ap.rearrange("p (k t) -> p k t", t=TB)
                        for kc in range(KC):
                            S.op("dve", lambda e: e.scalar_tensor_tensor(out=ystv[:, kc, :], in0=xcols(b, kc),
                                                                         scalar=par_t[:, P_NFIN + kc:P_NFIN + kc + 1], in1=fc.ap,
                                                                         op0=ALU.mult, op1=ALU.mult),
                                 reads=[xTb[b], par, fc], writes=[yst])
                        tk = S.dma("sp", yT_d[:, :, b * TB:(b + 1) * TB], ystv, reads=[yst])
                        out_tickets.append(tk)
                        for e_ in ("pe", "act", "dve"):
                            S._wait(e_, tk[0], tk[1])

            def plan():
                ada_left = [len(ada_todo)]

                def ada_plan(n):
                    for _ in range(n):
                        if ada_left[0] > 0:
                            ada_left[0] -= 1
                            wsched.append(("ada", None, None, 1))
                for l in range(L):
                    for b in (4, 3, 2):
                        for h in range(4):
                            wsched.append(("in", win_b[l, :, 5 * h + 2:5 * h + 4, :], conv[("in", l, h)], 2))
                            ada_plan(2)
                        for h in range(4):
                            wsched.append(("in", win_b[l, :, 20 + 4 * h + 1:20 + 4 * h + 3, :], conv[("in", l, 4 + h)], 2))
                            ada_plan(2)
                    for b in range(NBLK):
                        for h in range(4):
                            ada_plan(4)
                            wsched.append(("in", win_b[l, :, 5 * h:5 * h + 3, :], conv[("in", l, h)], 3))
                            wsched.append(("in", win_b[l, :, 5 * h + 3:5 * h + 5, :], conv[("in", l, h)], 2))
                        for h in range(4):
                            ada_plan(4)
                            wsched.append(("in", win_b[l, :, 20 + 4 * h:20 + 4 * h + 3, :], conv[("in", l, 4 + h)], 3))
                            wsched.append(("in", win_b[l, :, 20 + 4 * h + 3:20 + 4 * h + 4, :], conv[("in", l, 4 + h)], 1))
                        for g in range(3):
                            t0, n = 3 * g, (3 if g < 2 else 2)
                            wsched.append(("out", wout_b[l, :, t0:t0 + n, :], conv[("out", l, 0)], n))
                        for g in range(22):
                            wsched.append(("up", wup_b[l, :, 2 * g:2 * g + 2, :], conv[("up", l, (2 * g) // 11)], 2))
                        for g in range(8):
                            wsched.append(("dn", wdn_b[l, :, 2 * g:2 * g + 2, :], conv[("dn", l, g)], -1))
                it = iter(list(ada_todo))
                for i, wsc in enumerate(wsched):
                    if wsc[0] == "ada" and wsc[1] is None:
                        l_, j_ = next(it)
                        wsched[i] = ("ada", adaw_b[l_, :, j_, :], conv[("ada", l_, j_ // 8)], 1)

            for j in range(16):
                wsched.append(("ada32", adaw_d[0, :, j, :], None, 1))
            plan()
            S.dma("sp", xTb[4].ap, xT_d[:, :, 4 * TB:5 * TB], writes=[xTb[4]])
            for comp in range(2, 6):
                S.dma("pool", adaw_b[0, :, comp * 8:comp * 8 + 8, :], adaw_d[0, :, comp * 8:comp * 8 + 8, :],
                      writes=[conv[("ada", 0, comp)]])
            for j in range(16):
                ada_tile(0, j, fp32=True)
            for b in (3, 2, 0, 1):
                S.dma("sp", xTb[b].ap, xT_d[:, :, b * TB:(b + 1) * TB], writes=[xTb[b]])
            deferred = {"l0_rest": False}
            deferred_conv = []

            def pop_conv():
                if deferred_conv:
                    ob, ib, cb = deferred_conv.pop(0)
                    S.dma("pool", ob, ib, writes=[cb])
            ckpt("ada", lambda: (small_t[:, :], [small]))
            for l in range(L):
                ret_consts(l)
                pass1(l)
                ckpt("p1_%d" % l, lambda: (Sc_t[:, :, :].rearrange("p a b -> p (a b)"), Sc))
                pass2(l, l == L - 1)
                ckpt("l%d" % l, lambda: (xT[:, 0, :], xTb))
            assert wstate["used"] == len(wsched), (wstate, len(wsched))
            assert not ada_todo

        try:
            body()
        except _Stop:
            pass
        last = {}
        for k, v in out_tickets:
            last[k] = max(last.get(k, 0), v)
        for k, v in last.items():
            S._wait("sp", k, v)
        for q in S.dpool:
            for k in S.dpool[q]:
                if S.val[k] > 0:
                    S._wait("sp", k, S.val[k])
        print("bass ops:", S.nops)
    return nc


def _tile_w(W, col_starts):
    K = W.shape[0]
    Wr = W.reshape(K // 128, 128, W.shape[1])
    out = np.empty((128, len(col_starts), K // 128, 128), np.float32)
    for i, c0 in enumerate(col_starts):
        out[:, i] = Wr[:, :, c0:c0 + 128].transpose(1, 0, 2)
    return out


def _consts():
    cst = np.zeros((128, NCST), np.float32)
    t = np.arange(512)
    cst[:, C_M32:C_M32 + 512] = (t % 32 != 0).astype(np.float32)[None, :]
    cst[:, C_M64:C_M64 + 512] = (t % 64 != 0).astype(np.float32)[None, :]
    s = np.arange(128)[:, None]
    tt = np.arange(128)[None, :]
    cst[:, C_MASK:C_MASK + 128] = ((s // 64 == tt // 64) & (s <= tt)).astype(np.float32)
    cst[:, C_ID:C_ID + 128] = np.eye(128, dtype=np.float32)
    R = np.zeros((128, 128), np.float32)
    for base in (0, 64):
        for i in range(32):
            R[base + i, base + i + 32] = -1.0
            R[base + 32 + i, base + i] = 1.0
    cst[:, C_RM:C_RM + 128] = R.T
    cst[:, C_TP1:C_TP1 + 64] = (np.arange(64) + 1).astype(np.float32)[None, :]
    cst[:, C_TM:C_TM + 64] = (63 - np.arange(64)).astype(np.float32)[None, :]
    pos = np.arange(SEQS)
    rows = (pos // 64).astype(np.float32)
    cols = (pos % 64).astype(np.float32)
    inv = (10000.0 ** (-np.arange(32, dtype=np.float32) / 32)).astype(np.float32)
    rope = np.zeros((128, 2, SEQS), np.float32)
    for dd in range(128):
        p = rows if dd < 64 else cols
        ang = (p * inv[dd % 32]).astype(np.float32)
        rope[dd, 0] = np.cos(ang)
        rope[dd, 1] = np.sin(ang)
    return cst, rope


_CACHE = {}


def kernel(x_prompt, x_sample, state_hgrn, state_ret, c, c_ctx, ada_w, ada_b, norm_mix, norm_ffn, w_in,
           hgrn_lb_logits, hgrn_norm, ret_decay_logit, ret_norm, w_out, w_up, w_down, norm_final, _prepare_only=False):
    f32 = lambda a: np.ascontiguousarray(np.asarray(a), dtype=np.float32)
    x_prompt, x_sample, state_hgrn, state_ret = f32(x_prompt), f32(x_sample), f32(state_hgrn), f32(state_ret)
    c, c_ctx, ada_w, ada_b = f32(c), f32(c_ctx), f32(ada_w), f32(ada_b)
    norm_mix, norm_ffn, w_in, hgrn_lb_logits = f32(norm_mix), f32(norm_ffn), f32(w_in), f32(hgrn_lb_logits)
    hgrn_norm, ret_decay_logit, ret_norm = f32(hgrn_norm), f32(ret_decay_logit), f32(ret_norm)
    w_out, w_up, w_down, norm_final = f32(w_out), f32(w_up), f32(w_down), f32(norm_final)

    if "cst" not in _CACHE:
        _CACHE["cst"] = _consts()
    cst, rope = _CACHE["cst"]

    in_cols = []
    for h in range(4):
        in_cols += [0 + 128 * h, 512 + 128 * h, 1024 + 128 * h, 1536 + 128 * h, 2048 + 128 * h]
    for h in range(4):
        in_cols += [2560 + 128 * h, 3072 + 128 * h, 3584 + 128 * h, 4096 + 128 * h]
    up_cols = []
    for j in range(NFF):
        up_cols += [128 * j, DFF + 128 * j]
    adaw_t = np.stack([_tile_w(ada_w[l], [128 * j for j in range(48)]).reshape(128, 48, 1024) for l in range(L)])
    win_t = np.stack([_tile_w(w_in[l], in_cols).reshape(128, 36, 1024) for l in range(L)])
    wout_t = np.stack([_tile_w(w_out[l], [128 * j for j in range(8)]).reshape(128, 8, 1024) for l in range(L)])
    wup_t = np.stack([_tile_w(w_up[l], up_cols).reshape(128, 44, 1024) for l in range(L)])
    wdn_t = np.stack([_tile_w(w_down[l], [128 * j for j in range(8)]).reshape(128, 16, 1408) for l in range(L)])

    def col128(v):
        return v.reshape(-1, 128).T

    in_maps = []
    for i in range(NCORES):
        xs = [x_prompt[2 * i].T, x_prompt[2 * i + 1].T, x_sample[i].T]
        xTm = np.concatenate(xs, axis=1)
        xTm = np.ascontiguousarray(xTm.reshape(KC, 128, T).transpose(1, 0, 2))
        par = np.zeros((128, NPAR), np.float32)
        conds = np.stack([c_ctx, c[i]], axis=0)
        par[:, P_COND:P_COND + 16] = conds.reshape(2, KC, 128).transpose(2, 1, 0).reshape(128, 16)
        for l in range(L):
            par[:, P_ADAB + 48 * l:P_ADAB + 48 * (l + 1)] = col128(ada_b[l])
            par[:, P_NMIX + 8 * l:P_NMIX + 8 * (l + 1)] = col128(norm_mix[l])
            par[:, P_NFFN + 8 * l:P_NFFN + 8 * (l + 1)] = col128(norm_ffn[l])
            for d in range(2):
                par[:, P_LB + (l * 2 + d) * 4:P_LB + (l * 2 + d) * 4 + 4] = col128(hgrn_lb_logits[l, d])
                par[:, P_RD + (l * 2 + d) * 4:P_RD + (l * 2 + d) * 4 + 4] = ret_decay_logit[l, d][None, :]
            par[:, P_HN + 4 * l:P_HN + 4 * (l + 1)] = col128(hgrn_norm[l])
            par[:, P_RN + 4 * l:P_RN + 4 * (l + 1)] = col128(ret_norm[l])
        par[:, P_NFIN:P_NFIN + 8] = col128(norm_final)
        sh = state_hgrn[i].reshape(L * 2, H, 128, 128)
        sr = state_ret[i].reshape(L * 2, H, 128, 128)
        s0 = np.concatenate([sh, sr], axis=1).reshape(L * 2 * 8, 128, 128).transpose(1, 0, 2)
        in_maps.append({
            "xT": xTm, "params": par, "cst": cst, "rope": rope, "s0": np.ascontiguousarray(s0),
            "ada_w": adaw_t, "w_in": win_t, "w_out": wout_t, "w_up": wup_t, "w_down": wdn_t,
        })
    if _prepare_only:
        return in_maps
    if "nc" not in _CACHE:
        _CACHE["nc"] = build_program()
    nc = _CACHE["nc"]
    res = run_bass_kernel_spmd(nc, in_maps, core_ids=list(range(NCORES)))

    y_prompt = np.empty((16, SEQP, D), np.float32)
    y_sample = np.empty((8, SEQS, D), np.float32)
    nsh = np.empty((16, L, 2, H, 128, 128), np.float32)
    nsr = np.empty((16, L, 2, H, 128, 128), np.float32)
    for i in range(NCORES):
        r = res.results[i]
        yT = np.asarray(r["yT"])
        y = yT.transpose(2, 1, 0).reshape(T, D)
        y_prompt[2 * i] = y[0:256]
        y_prompt[2 * i + 1] = y[256:512]
        y_sample[i] = y[512:]
        st = np.asarray(r["st"]).reshape(128, 2, L, 2, 2, H, 128)
        st = st.transpose(1, 2, 3, 4, 5, 0, 6)
        for j in range(2):
            nsh[2 * i + j] = st[j, :, :, 0]
            nsr[2 * i + j] = st[j, :, :, 1]
    return (y_prompt, y_sample, nsh, nsr)
```
